# Optimizing a Trainium2 kernel written in Bass

```python
import jax, jax.numpy as jnp
from jax import lax
import numpy as np

D_MODEL = 1024
BATCH = 32
SEQ = 2048
DEPTH = 2

CHUNK = 64
EPS = 1e-6
NEG_INF = -1e30

CONV_WIDTH = D_MODEL // 2
CONV_KERNEL = 31
HEAD_DIM = 64
ATTN_HEADS = D_MODEL // 128
ATTN_WIDTH = ATTN_HEADS * HEAD_DIM
LEFT_CHUNKS = 8
BAND = (LEFT_CHUNKS + 1) * CHUNK
KEY_PAD = BAND - CHUNK
MAX_REL = 256
POOL_WINDOWS = (2, 4, 8, 16)
POOL_GROUPS = len(POOL_WINDOWS)
POOL_WIDTH = D_MODEL // 2
POOL_GROUP_DIM = POOL_WIDTH // POOL_GROUPS
N_BRANCH = 3

IN_SPLITS = (CONV_WIDTH, CONV_WIDTH, CONV_WIDTH,
             ATTN_WIDTH, ATTN_WIDTH, ATTN_WIDTH, ATTN_WIDTH,
             POOL_WIDTH, POOL_WIDTH,
             N_BRANCH * D_MODEL)
IN_COLS = sum(IN_SPLITS)

kernel_name = "hybrid_conv_chunkattn_pool_gated_block"


def rms_norm(x, g):
    xf = x.astype(jnp.float32)
    y = xf * lax.rsqrt(jnp.mean(xf * xf, axis=-1, keepdims=True) + EPS)
    return (y * g.astype(jnp.float32)).astype(x.dtype)


def layer_norm(x, g, b):
    xf = x.astype(jnp.float32)
    mu = jnp.mean(xf, axis=-1, keepdims=True)
    xc = xf - mu
    y = xc * lax.rsqrt(jnp.mean(xc * xc, axis=-1, keepdims=True) + EPS)
    return (y * g.astype(jnp.float32) + b.astype(jnp.float32)).astype(x.dtype)


def conv_branch(a, b, gate, dw, dw_b, ln_g, ln_b, w_o):
    u = a * jax.nn.sigmoid(b)
    u = lax.conv_general_dilated(
        u, dw[:, None, :], window_strides=(1,),
        padding=[(CONV_KERNEL - 1, 0)],
        dimension_numbers=('NWC', 'WIO', 'NWC'),
        feature_group_count=CONV_WIDTH) + dw_b
    u = jax.nn.silu(layer_norm(u, ln_g, ln_b))
    return (u * jax.nn.silu(gate)) @ w_o


def chunk_attention_branch(q, k, v, gate, rel_table, w_o):
    B, S, _ = q.shape
    n_chunks = S // CHUNK
    qc = (q * (HEAD_DIM ** -0.5)).reshape(B, n_chunks, CHUNK, ATTN_HEADS, HEAD_DIM)
    qc = qc.transpose(1, 0, 3, 2, 4)
    kh = k.reshape(B, S, ATTN_HEADS, HEAD_DIM).transpose(0, 2, 1, 3)
    vh = v.reshape(B, S, ATTN_HEADS, HEAD_DIM).transpose(0, 2, 1, 3)
    kh = jnp.pad(kh, ((0, 0), (0, 0), (KEY_PAD, 0), (0, 0)))
    vh = jnp.pad(vh, ((0, 0), (0, 0), (KEY_PAD, 0), (0, 0)))
    rel = jnp.arange(CHUNK)[:, None] + KEY_PAD - jnp.arange(BAND)[None, :]
    bias = rel_table[:, jnp.clip(rel, -MAX_REL, MAX_REL) + MAX_REL].astype(jnp.float32)
    key_offsets = jnp.arange(BAND) - KEY_PAD

    def one_chunk(args):
        q_blk, c = args
        start = c * CHUNK
        kb = lax.dynamic_slice_in_dim(kh, start, BAND, axis=2)
        vb = lax.dynamic_slice_in_dim(vh, start, BAND, axis=2)
        s = jnp.einsum('bhqd,bhkd->bhqk', q_blk, kb).astype(jnp.float32) + bias
        valid = (start + key_offsets) >= 0
        s = jnp.where(valid, s, NEG_INF)
        p = jax.nn.softmax(s, axis=-1).astype(vb.dtype)
        return jnp.einsum('bhqk,bhkd->bhqd', p, vb)

    o = lax.map(one_chunk, (qc, jnp.arange(n_chunks)))
    o = o.transpose(1, 0, 3, 2, 4).reshape(B, S, ATTN_WIDTH)
    return (o * jax.nn.silu(gate)) @ w_o


def pool_branch(u, gate, w_grp, b_grp, scale, w_o):
    B, S, _ = u.shape
    uf = u.astype(jnp.float32)
    cs = jnp.pad(jnp.cumsum(uf, axis=1), ((0, 0), (1, 0), (0, 0)))
    t = jnp.arange(S)
    outs = []
    for g, w in enumerate(POOL_WINDOWS):
        sl = slice(g * POOL_GROUP_DIM, (g + 1) * POOL_GROUP_DIM)
        csg = cs[..., sl]
        lower = jnp.concatenate(
            [jnp.zeros((B, w - 1, POOL_GROUP_DIM), jnp.float32), csg[:, :S + 1 - w]], axis=1)
        cnt = jnp.minimum(t + 1, w).astype(jnp.float32)[None, :, None]
        outs.append((csg[:, 1:] - lower) / cnt - uf[..., sl])
    pooled = jnp.stack(outs, axis=2).astype(u.dtype)
    mixed = jnp.einsum('bsgc,gcd->bsgd', pooled, w_grp) + b_grp
    mixed = mixed.reshape(B, S, POOL_WIDTH) * scale
    return (mixed * jax.nn.silu(gate)) @ w_o


def hybrid_layer(x, pre_g, post_g, w_in, conv_dw, conv_dw_b, conv_ln_g, conv_ln_b,
                 w_conv_out, rel_bias, w_attn_out, pool_w, pool_b, pool_scale,
                 w_pool_out, w_out):
    B, S, D = x.shape
    h = rms_norm(x, pre_g)
    z = h @ w_in
    (c_a, c_b, c_gate, q, k, v, a_gate, p_in, p_gate, g_merge) = jnp.split(
        z, np.cumsum(IN_SPLITS)[:-1].tolist(), axis=-1)
    y_conv = conv_branch(c_a, c_b, c_gate, conv_dw, conv_dw_b, conv_ln_g, conv_ln_b, w_conv_out)
    y_attn = chunk_attention_branch(q, k, v, a_gate, rel_bias, w_attn_out)
    y_pool = pool_branch(p_in, p_gate, pool_w, pool_b, pool_scale, w_pool_out)
    gates = jax.nn.sigmoid(g_merge).reshape(B, S, N_BRANCH, D)
    merged = gates[:, :, 0] * y_conv + gates[:, :, 1] * y_attn + gates[:, :, 2] * y_pool
    y = merged @ w_out
    return x + rms_norm(y, post_g)


def setup_inputs(seed: int = 0) -> dict:
    key = jax.random.key(seed)
    ks = jax.random.split(key, 20)
    n = lambda k, shape, s: jax.random.normal(k, shape, jnp.float32) * s
    L, D = DEPTH, D_MODEL
    return {
        "x": n(ks[0], (BATCH, SEQ, D), 1.0),
        "pre_norm_g": 1.0 + n(ks[1], (L, D), 0.05),
        "post_norm_g": 1.0 + n(ks[2], (L, D), 0.05),
        "w_in": n(ks[3], (L, D, IN_COLS), D ** -0.5),
        "conv_dw": n(ks[4], (L, CONV_KERNEL, CONV_WIDTH), CONV_KERNEL ** -0.5),
        "conv_dw_b": n(ks[5], (L, CONV_WIDTH), 0.02),
        "conv_ln_g": 1.0 + n(ks[6], (L, CONV_WIDTH), 0.05),
        "conv_ln_b": n(ks[7], (L, CONV_WIDTH), 0.02),
        "w_conv_out": n(ks[8], (L, CONV_WIDTH, D), CONV_WIDTH ** -0.5),
        "rel_bias": n(ks[9], (L, ATTN_HEADS, 2 * MAX_REL + 1), 0.1),
        "w_attn_out": n(ks[10], (L, ATTN_WIDTH, D), ATTN_WIDTH ** -0.5),
        "pool_w": n(ks[11], (L, POOL_GROUPS, POOL_GROUP_DIM, POOL_GROUP_DIM), POOL_GROUP_DIM ** -0.5),
        "pool_b": n(ks[12], (L, POOL_GROUPS, POOL_GROUP_DIM), 0.02),
        "pool_scale": 1.0 + n(ks[13], (L, POOL_WIDTH), 0.1),
        "w_pool_out": n(ks[14], (L, POOL_WIDTH, D), POOL_WIDTH ** -0.5),
        "w_out": n(ks[15], (L, D, D), D ** -0.5),
    }


def reference(x, pre_norm_g, post_norm_g, w_in, conv_dw, conv_dw_b, conv_ln_g, conv_ln_b,
              w_conv_out, rel_bias, w_attn_out, pool_w, pool_b, pool_scale, w_pool_out, w_out):
    for l in range(DEPTH):
        x = hybrid_layer(x, pre_norm_g[l], post_norm_g[l], w_in[l], conv_dw[l], conv_dw_b[l],
                         conv_ln_g[l], conv_ln_b[l], w_conv_out[l], rel_bias[l], w_attn_out[l],
                         pool_w[l], pool_b[l], pool_scale[l], w_pool_out[l], w_out[l])
    return x
```

```python
import contextlib
import numpy as np
import concourse.bass as bass
import concourse.mybir as mybir
from concourse.bass_utils import run_bass_kernel_spmd

F32 = mybir.dt.float32
BF16 = mybir.dt.bfloat16
AF = mybir.ActivationFunctionType
ALU = mybir.AluOpType

PE, ACT, DVE, POOL, SP = "pe", "act", "dve", "pool", "sp"
ENGINES = (PE, ACT, DVE, POOL, SP)
SEM_CAP = 30000

L = 2
D = 1024
T = 512
NU = 27
NPV = 36
NRING = 4
EPS = 1e-6
NEG = -30000.0
N_CORES = 8
DEBUG_STOP = 99


class _Op:
    __slots__ = ("eng", "fn", "deps", "dma", "stream", "idx", "sig", "tok")

    def __init__(self, eng, fn, deps, dma, stream, idx):
        self.eng = eng
        self.fn = fn
        self.deps = deps
        self.dma = dma
        self.stream = stream
        self.idx = idx
        self.sig = False
        self.tok = None


class Sched:
    def __init__(self):
        self.ops = []
        self.last_writer = {}
        self.readers = {}
        self.stream_concurrent = {}

    def add(self, eng, fn, reads=(), writes=(), dma=False, stream=None, concurrent=False):
        idx = len(self.ops)
        deps = set()
        for r in reads:
            w = self.last_writer.get(r)
            if w is not None:
                deps.add((w, "raw"))
        for r in writes:
            w = self.last_writer.get(r)
            if w is not None:
                deps.add((w, "waw"))
            for rd in self.readers.get(r, ()):
                deps.add((rd, "war"))
        for r in reads:
            if isinstance(r, tuple) and r[0] == "ps" and r not in writes:
                for rd in self.readers.get(r, ()):
                    deps.add((rd, "war"))
            self.readers.setdefault(r, []).append(idx)
        for r in writes:
            self.last_writer[r] = idx
            self.readers[r] = []
        if dma:
            self.stream_concurrent[stream] = concurrent or self.stream_concurrent.get(stream, False)
        self.ops.append(_Op(eng, fn, deps, dma, stream, idx))
        return idx

    def emit(self, nc, final_wait_streams=()):
        ops = self.ops
        need = {}
        for op in ops:
            lst = {}
            for (p, kind) in op.deps:
                prod = ops[p]
                if (not prod.dma) and (not op.dma) and prod.eng == op.eng:
                    if op.eng != PE:
                        lst[p] = True
                    continue
                lst[p] = True
            need[op.idx] = list(lst.keys())
            for p in lst:
                ops[p].sig = True
        eng_count = {e: 0 for e in ENGINES}
        stream_count = {}
        for op in ops:
            if op.dma:
                c = stream_count.get(op.stream, 0) + 16
                stream_count[op.stream] = c
                op.tok = (("dma", op.stream), c)
            elif op.sig:
                n = eng_count[op.eng]
                eng_count[op.eng] = n + 1
                op.tok = (("eng", op.eng, n // SEM_CAP), n % SEM_CAP + 1)
        for op in ops:
            if op.dma and self.stream_concurrent[op.stream]:
                op.tok = (op.tok[0], stream_count[op.stream])
        semkeys = []
        seen = set()
        for op in ops:
            if op.tok is not None and op.tok[0] not in seen:
                seen.add(op.tok[0])
                semkeys.append(op.tok[0])
        self.n_sems = len(semkeys)
        with contextlib.ExitStack() as st:
            sems = {}
            for i, k in enumerate(semkeys):
                sems[k] = st.enter_context(nc.semaphore("s%d" % i))
            block = st.enter_context(nc.Block())
            per_eng = {e: [] for e in ENGINES}
            for op in ops:
                per_eng[op.eng].append(op)

            def run_engine(e, handle):
                waited = {}
                for op in per_eng[e]:
                    w = {}
                    for p in need[op.idx]:
                        k, v = ops[p].tok
                        if v > w.get(k, 0):
                            w[k] = v
                    for k, v in w.items():
                        if waited.get(k, 0) >= v:
                            continue
                        handle.wait_ge(sems[k], v)
                        waited[k] = v
                    ins = op.fn(handle)
                    if op.tok is not None:
                        ins.then_inc(sems[op.tok[0]], 16 if op.dma else 1)
                if e == SP:
                    for s in final_wait_streams:
                        handle.wait_ge(sems[("dma", s)], stream_count[s])

            if per_eng[PE]:
                block.tensor(lambda hd: run_engine(PE, hd))
            if per_eng[ACT]:
                block.scalar(lambda hd: run_engine(ACT, hd))
            if per_eng[DVE]:
                block.vector(lambda hd: run_engine(DVE, hd))
            if per_eng[POOL]:
                block.gpsimd(lambda hd: run_engine(POOL, hd))
            block.sync(lambda hd: run_engine(SP, hd))


PV_PRE, PV_POST, PV_DWB, PV_LNG, PV_LNB, PV_PB, PV_PS = 0, 8, 16, 20, 24, 28, 32
WIN_UNITS = [512, 0, 1024, 1536, 2048, 2560, 3072, 3584, 4096]
MERGE_ORDER = (0, 2, 1)


def unit_cols(u):
    if 9 <= u <= 12:
        return 31 * 128
    if 13 <= u <= 24 and (u - 13) % 2 == 1:
        return 4 * 512
    return 4096


def build_program(NSEQ, NT, layers=(0, 1)):
    nc = bass.Bass("TRN2", target_bir_lowering=False)
    SL = NT * T

    def din(name, shape):
        return nc.dram_tensor(name, shape, F32, kind="ExternalInput").ap()

    xT = din("xT", [NSEQ, D, SL])
    w_in = din("w_in", [L, D, 7680])
    w_bo = [din("w_co", [L, 512, D]), din("w_ao", [L, 512, D]), din("w_po", [L, 512, D])]
    w_out = din("w_out", [L, D, D])
    pool_w = din("pool_w", [L, 4, 128, 128])
    pvec = din("pvec", [128, L, NPV])
    dwT = din("dwT", [128, L, 4, 31])
    biasT = din("biasT", [128, L, 8, 640])
    ident = din("ident", [128, 128])
    invcnt = din("invcnt", [128, 16])
    yT = nc.dram_tensor("yT", [NSEQ, D, SL], F32, kind="ExternalOutput").ap()
    wscr = nc.dram_tensor("wscr", [L, NU, 128, 4096], BF16).ap()
    bscr = nc.dram_tensor("bscr", [L, 128, 8 * 640], BF16).ap()

    S = Sched()
    with contextlib.ExitStack() as st:
        def sb(name, shape, dt):
            return st.enter_context(nc.sbuf_tensor("sb_" + name, shape, dt))

        xbuf = [sb("x%d" % i, [128, 8, T], F32) for i in range(2)]
        h = sb("h", [128, 8, T], BF16)
        mrg = sb("mrg", [128, 8, T], BF16)
        u_ext = sb("u", [128, 4, T + 30], BF16)
        cg = sb("cg", [128, 4, T], BF16)
        qT = sb("qT", [128, 4, T], BF16)
        kT = [sb("kT%d" % l, [128, 4, 2, T], BF16) for l in range(L)]
        Vb = [sb("V%d" % l, [128, 2, 4, 4, 192], BF16) for l in range(L)]
        ag = sb("ag", [128, 4, T], BF16)
        p_ext = sb("p", [128, 4, T + 15], F32)
        pg = sb("pg", [128, 4, T], BF16)
        big = sb("big", [128, 8, T], F32)
        cbo = sb("cbo", [128, 4, T], BF16)
        pmb = sb("pmb", [128, 4, T], BF16)
        PT = [sb("PT%d" % i, [128, T], BF16) for i in range(4)]
        rden = [sb("rden%d" % i, [128, T], F32) for i in range(2)]
        pa = sb("pa", [128, T + 15], F32)
        pb = sb("pb", [128, T + 15], F32)
        sg = [sb("sg%d" % i, [128, T], F32) for i in range(2)]
        rs = sb("rs", [128, T], F32)
        rs2 = sb("rs2", [128, T], F32)
        ring = [sb("ring%d" % i, [128, 4096], BF16) for i in range(NRING)]
        bias_sb = sb("bias", [128, 8, 640], BF16)
        ident_sb = sb("ident", [128, 128], F32)
        ones_bf = sb("ones", [128, 128], BF16)
        pv = sb("pv", [128, L, NPV], F32)
        dwT_sb = sb("dwT", [128, L, 4, 31], F32)
        pw_bf = sb("pw", [128, L, 4, 128], BF16)
        invc = sb("invc", [128, 16], F32)
        cst = sb("cst", [128, 4], F32)
        uh = [sb("uh%d" % l, [128, 4, 30], BF16) for l in range(L)]
        ph = [sb("ph%d" % l, [128, 4, 15], F32) for l in range(L)]
        ps = [st.enter_context(nc.psum_tensor("ps%d" % i, [128, T], F32)) for i in range(8)]

        def dma(eng, out, in_, reads, writes, stream, concurrent=False):
            S.add(eng, lambda e: e.dma_start(out=out, in_=in_), reads, writes, dma=True,
                  stream=stream, concurrent=concurrent)

        def mm(out, lhsT, rhs, start, stop, reads, writes, **kw):
            S.add(PE, lambda e: e.matmul(out, lhsT, rhs, start=start, stop=stop, **kw), reads, writes)

        def act(out, in_, func, reads, writes, bias=None, scale=None):
            kw = {}
            if bias is not None:
                kw["bias"] = bias
            if scale is not None:
                kw["scale"] = scale
            S.add(ACT, lambda e: e.activation(out=out, in_=in_, func=func, **kw), reads, writes)

        def tt(eng, out, in0, in1, op, reads, writes):
            S.add(eng, lambda e: e.tensor_tensor(out=out, in0=in0, in1=in1, op=op), reads, writes)

        def ts(eng, out, in0, s1, s2, op0, op1, reads, writes):
            if s2 is None:
                S.add(eng, lambda e: e.tensor_scalar(out=out, in0=in0, scalar1=s1, scalar2=None, op0=op0), reads, writes)
            else:
                S.add(eng, lambda e: e.tensor_scalar(out=out, in0=in0, scalar1=s1, scalar2=s2, op0=op0, op1=op1), reads, writes)

        def stt(eng, out, in0, scalar, in1, op0, op1, reads, writes):
            S.add(eng, lambda e: e.scalar_tensor_tensor(out=out, in0=in0, scalar=scalar, in1=in1, op0=op0, op1=op1),
                  reads, writes)

        def cp(eng, out, in_, reads, writes):
            S.add(eng, lambda e: e.tensor_copy(out=out, in_=in_), reads, writes)

        def mset(eng, ap, val, writes):
            S.add(eng, lambda e: e.memset(ap, val), (), writes)

        def recip(out, in_, reads, writes):
            S.add(DVE, lambda e: e.reciprocal(out=out, in_=in_), reads, writes)

        state = {"bk": 0, "sg": 0, "cursor": 0, "loaded": 0, "diag_j": 0, "prenormed": set(), "sbc": 0}

        def bank():
            state["bk"] = (state["bk"] + 1) % 8
            return state["bk"]

        def bank03():
            state["sbc"] += 1
            return state["sbc"] % 4

        def bank4():
            state["bk"] = (state["bk"] + 1) % 8
            if state["bk"] in (4, 5):
                state["bk"] = 6
            return state["bk"]

        dma(SP, ident_sb[:], ident, [], [("c", "ident")], "const", True)
        dma(SP, pv[:], pvec, [], [("c", "pv")], "const", True)
        dma(SP, dwT_sb[:], dwT, [], [("c", "dwT")], "const", True)
        dma(SP, invc[:], invcnt, [], [("c", "invc")], "const", True)
        mset(DVE, ones_bf[:], 1.0, [("c", "ones")])
        for l in layers:
            for sg_ in range(2):
                mset(POOL, Vb[l][:, sg_, :, :, 64:128], 1.0, [("vones", l)])
        mset(DVE, cst[:, 0:1], EPS, [("c", "cst")])
        mset(DVE, cst[:, 1:2], EPS, [("c", "cst")])
        def cast_unit(l, u, src, kc_n, grp):
            n = kc_n * 512
            dma(POOL, wscr[l, u][:, 0:n].rearrange("p (k c) -> p k c", k=kc_n),
                src.rearrange("(k p) c -> p k c", p=128), [], [("wscr", l, u)], ("cast", l, grp), True)

        def bias_setup():
            dma(POOL, pw_bf[:], pool_w.rearrange("l g c d -> c l g d"), [], [("c", "pw")], "pwld", True)
            for l in layers:
                dma(POOL, bias_sb[:], biasT[:, l], [], [("bias",)], "biasld")
                mset(POOL, bias_sb[64:128, :, 0:64], NEG, [("bias",)])
                mset(POOL, bias_sb[0:64, :, 576:640], NEG, [("bias",)])
                dma(SP, bscr[l].rearrange("p (h j) -> p h j", h=8), bias_sb[:], [("bias",)], [("bscr", l)], "biasst")

        WGRP = {0: 0, 2: 1, 7: 1, 1: 2, 3: 2, 4: 3, 5: 3, 6: 4, 8: 4}
        for l in layers:
            for u in [0, 2, 7, 1, 3, 4, 5, 6, 8]:
                c0 = WIN_UNITS[u]
                cast_unit(l, u, w_in[l][:, c0:c0 + 512], 8, WGRP[u])
            if l == layers[0]:
                bias_setup()
            for mgi in range(2):
                for bi, b in enumerate(MERGE_ORDER):
                    u = 13 + (mgi * 3 + bi) * 2
                    c0 = 4608 + b * 1024 + mgi * 512
                    cast_unit(l, u, w_in[l][:, c0:c0 + 512], 8, 5 + mgi)
                    cast_unit(l, u + 1, w_bo[b][l][:, mgi * 512:(mgi + 1) * 512], 4, 5 + mgi)
            for hf in range(2):
                cast_unit(l, 25 + hf, w_out[l][:, hf * 512:(hf + 1) * 512], 8, 7)
        for l in layers:
            for c in range(4):
                s_ = state["diag_j"] % NRING
                state["diag_j"] += 1
                for k in range(31):
                    ts(DVE, ring[s_][:, k * 128:(k + 1) * 128], ident_sb[:], dwT_sb[:, l, c, k:k + 1], None,
                       ALU.mult, None, [("c", "ident"), ("c", "dwT")], [("ring", s_)])
                dma(SP, wscr[l, 9 + c][:, 0:3968], ring[s_][:, 0:3968], [("ring", s_)], [("wscr", l, 9 + c)],
                    ("diagst", s_))

        tiles = [(s, t) for s in range(NSEQ) for t in range(NT)]
        UORDER = [0, 2, 7, 1, 9, 10, 11, 12, 3, 4, 5, 6, 8] + list(range(13, NU))
        unit_seq = [(l, u) for _ in tiles for l in layers for u in UORDER]

        def emit_loads(upto):
            while state["loaded"] <= min(upto, len(unit_seq) - 1):
                i = state["loaded"]
                l, u = unit_seq[i]
                s_ = i % NRING
                n = unit_cols(u)
                dma(SP, ring[s_][:, 0:n], wscr[l, u][:, 0:n], [("wscr", l, u)], [("ring", s_)], ("ring", s_))
                state["loaded"] += 1

        def use_unit(l, u, extra=0):
            i = state["cursor"]
            assert unit_seq[i] == (l, u), (unit_seq[i], l, u)
            emit_loads(i + NRING - 1 - extra)
            state["cursor"] += 1
            return i % NRING

        def load_x(ti):
            s, t = tiles[ti]
            b = ti % 2
            dma(SP, xbuf[b][:], xT[s][:, t * T:(t + 1) * T].rearrange("(c p) t -> p c t", p=128),
                [], [("x", b, c) for c in range(8)], ("xld", b))

        def prenorm_sq(l, ti, hoisted=False):
            xb_i = ti % 2
            xb = xbuf[xb_i]
            X = lambda c: ("x", xb_i, c)
            sq = [(cbo[:, c, :], ("cbo", c)) for c in range(4)] + [(pmb[:, c, :], ("pmb", c)) for c in range(4)]
            for c in range(8):
                if c % 2 == 0 or not hoisted:
                    act(sq[c][0], xb[:, c, :], AF.Square, [X(c)], [sq[c][1]])
                else:
                    tt(DVE, sq[c][0], xb[:, c, :], xb[:, c, :], ALU.mult, [X(c)], [sq[c][1]])

        def prenorm_fin(l, ti):
            state["prenormed"].add((l, ti))
            xb_i = ti % 2
            xb = xbuf[xb_i]
            X = lambda c: ("x", xb_i, c)
            sq = [(cbo[:, c, :], ("cbo", c)) for c in range(4)] + [(pmb[:, c, :], ("pmb", c)) for c in range(4)]
            bk = bank03()
            for c in range(8):
                mm(ps[bk][:], ones_bf[:], sq[c][0], c == 0, c == 7, [("c", "ones"), sq[c][1]], [("ps", bk)])
            act(rs2[:], ps[bk][:], AF.Ln, [("ps", bk), ("c", "cst")], [("rs2",)], bias=cst[:, 0:1], scale=1.0 / D)
            act(rs2[:], rs2[:], AF.Exp, [("rs2",)], [("rs2",)], scale=-0.5)
            for c in range(8):
                stt(DVE, h[:, c, :], xb[:, c, :], pv[:, l, PV_PRE + c:PV_PRE + c + 1], rs2[:],
                    ALU.mult, ALU.mult, [X(c), ("c", "pv"), ("rs2",)], [("h", c)])

        def prenorm(l, ti, hoisted=False):
            prenorm_sq(l, ti, hoisted)
            prenorm_fin(l, ti)

        def layer(l, ti, last_layer):
            s, t = tiles[ti]
            xb_i = ti % 2
            xb = xbuf[xb_i]
            seg = t % 2
            X = lambda c: ("x", xb_i, c)

            def racc(out, in_, reads, writes, expo=-1.0, scale=None, bias=None):
                act(out, in_, AF.Ln, reads, writes, bias=bias, scale=scale)
                act(out, out, AF.Exp, writes, writes, scale=expo)

            dma(SP, bias_sb[:], bscr[l].rearrange("p (h j) -> p h j", h=8), [("bscr", l)], [("bias",)], "biasld2")

            if (l, ti) not in state["prenormed"]:
                prenorm(l, ti)
            if DEBUG_STOP <= 1:
                if last_layer:
                    dma(SP, yT[s][:, t * T:(t + 1) * T].rearrange("(c p) t -> p c t", p=128), xb[:],
                        [X(c) for c in range(8)], [("y", ti)], ("st", xb_i))
                return

            def win_unit(u, evac):
                s_ = use_unit(l, u)
                for j in range(4):
                    bk = bank()
                    for kc in range(8):
                        mm(ps[bk][:], ring[s_][:, kc * 512 + j * 128:kc * 512 + (j + 1) * 128], h[:, kc, :],
                           kc == 0, kc == 7, [("ring", s_), ("h", kc)], [("ps", bk)])
                    evac(j, bk)

            cp(POOL, u_ext[:, :, 0:30], uh[l][:], [("uh", l)], [("u", c) for c in range(4)])
            cp(POOL, p_ext[:, :, 0:15], ph[l][:], [("ph", l)], [("p", g) for g in range(4)])

            win_unit(0, lambda j, bk: act(pmb[:, j, :], ps[bk][:], AF.Sigmoid, [("ps", bk)], [("pmb", j)]))
            win_unit(2, lambda j, bk: act(cg[:, j, :], ps[bk][:], AF.Silu, [("ps", bk)], [("cg", j)]))
            win_unit(7, lambda j, bk: act(p_ext[:, j, 15:15 + T], ps[bk][:], AF.Copy, [("ps", bk)], [("p", j)]))
            win_unit(1, lambda j, bk: tt(DVE, u_ext[:, j, 30:30 + T], ps[bk][:], pmb[:, j, :], ALU.mult,
                                         [("ps", bk), ("pmb", j)], [("u", j)]))

            for g in range(4):
                w = 2 << g
                cur, oth = pa, pb
                tt(POOL, cur[:, 1:T + 15], p_ext[:, g, 1:T + 15], p_ext[:, g, 0:T + 14], ALU.add,
                   [("p", g)], [("pa",)])
                cur_n, oth_n = "pa", "pb"
                sh = 2
                while sh < w:
                    lo = 2 * sh - 1
                    tt(POOL, oth[:, lo:T + 15], cur[:, lo:T + 15], cur[:, lo - sh:T + 15 - sh], ALU.add,
                       [(cur_n,)], [(oth_n,)])
                    cur, oth = oth, cur
                    cur_n, oth_n = oth_n, cur_n
                    sh *= 2
                stt(DVE, pmb[:, g, :], cur[:, 15:15 + T], 1.0 / w, p_ext[:, g, 15:15 + T], ALU.mult, ALU.subtract,
                    [(cur_n,), ("p", g)], [("pmb", g)])
                if t == 0:
                    tt(POOL, oth[:, 0:w - 1], cur[:, 15:15 + w - 1], invc[:, 0:w - 1], ALU.mult,
                       [(cur_n,), ("c", "invc")], [(oth_n,)])
                    tt(POOL, pmb[:, g, 0:w - 1], oth[:, 0:w - 1], p_ext[:, g, 15:15 + w - 1], ALU.subtract,
                       [(oth_n,), ("p", g), ("pmb", g)], [("pmb", g)])
            cp(POOL, ph[l][:], p_ext[:, :, T:T + 15], [("p", g) for g in range(4)], [("ph", l)])
            for c in range(4):
                s_ = use_unit(l, 9 + c)
                bk = bank()
                for k in range(31):
                    mm(ps[bk][:], ring[s_][:, k * 128:(k + 1) * 128], u_ext[:, c, k:k + T], k == 0, k == 30,
                       [("ring", s_), ("u", c)], [("ps", bk)])
                dwb = pv[:, l, PV_DWB + c:PV_DWB + c + 1]
                act(big[:, c, :], ps[bk][:], AF.Identity, [("ps", bk), ("c", "pv")], [("big", c)], bias=dwb)
                act(mrg[:, 4 + c, :], ps[bk][:], AF.Square, [("ps", bk), ("c", "pv")], [("mg", 4 + c)], bias=dwb)
                cp(POOL, mrg[:, c, :], big[:, c, :], [("big", c)], [("mg", c)])
            cp(POOL, uh[l][:], u_ext[:, :, T:T + 30], [("u", c) for c in range(4)], [("uh", l)])

            if False:
                if last_layer:
                    dma(SP, yT[s][:, t * T:(t + 1) * T].rearrange("(c p) t -> p c t", p=128), xb[:],
                        [X(c) for c in range(8)], [("y", ti)], ("st", xb_i))
                return
            bk1 = bank()
            for c in range(4):
                mm(ps[bk1][:], ones_bf[:], mrg[:, c, :], c == 0, c == 3, [("c", "ones"), ("mg", c)], [("ps", bk1)])
            bk2 = bank()
            for c in range(4):
                mm(ps[bk2][:], ones_bf[:], mrg[:, 4 + c, :], c == 0, c == 3, [("c", "ones"), ("mg", 4 + c)], [("ps", bk2)])

            ts(DVE, big[:, 4, :], ps[bk1][:], 1.0 / 512, None, ALU.mult, None, [("ps", bk1)], [("big", 4)])
            ts(DVE, big[:, 5, :], ps[bk2][:], 1.0 / 512, None, ALU.mult, None, [("ps", bk2)], [("big", 5)])
            if DEBUG_STOP <= 2:
                if last_layer:
                    dma(SP, yT[s][:, t * T:(t + 1) * T].rearrange("(c p) t -> p c t", p=128), xb[:],
                        [X(c) for c in range(8)], [("y", ti)], ("st", xb_i))
                return
            if ti + 1 < len(tiles) and l == layers[0]:
                load_x(ti + 1)
            win_unit(3, lambda j, bk: act(qT[:, j, :], ps[bk][:], AF.Copy, [("ps", bk)], [("q", j, 0), ("q", j, 1)],
                                          scale=0.125))
            win_unit(4, lambda j, bk: act(kT[l][:, j, seg, :], ps[bk][:], AF.Copy, [("ps", bk)], [("k", l, j, seg)]))
            s_ = use_unit(l, 5)
            for tb in range(4):
                bk = bank()
                for kc in range(8):
                    mm(ps[bk][:], h[:, kc, tb * 128:(tb + 1) * 128], ring[s_][:, kc * 512:(kc + 1) * 512],
                       kc == 0, kc == 7, [("ring", s_), ("h", kc)], [("ps", bk)])
                pv4 = ps[bk][:].rearrange("p (i two d) -> p i two d", two=2, d=64)
                act(Vb[l][:, seg, tb, :, 0:64], pv4[:, :, 0, :], AF.Copy, [("ps", bk), ("vones", l)], [("v", l, seg, tb)])
                act(Vb[l][:, seg, tb, :, 128:192], pv4[:, :, 1, :], AF.Copy, [("ps", bk), ("vones", l)], [("v", l, seg, tb)])

            tt(DVE, big[:, 6, :], big[:, 4, :], big[:, 4, :], ALU.mult, [("big", 4)], [("big", 6)])
            stt(DVE, big[:, 5, :], big[:, 5, :], EPS, big[:, 6, :], ALU.add, ALU.subtract, [("big", 5), ("big", 6)], [("big", 5)])
            racc(big[:, 6, :], big[:, 5, :], [("big", 5)], [("big", 6)], expo=-0.5)
            for c in range(4):
                e1 = DVE if c % 2 == 0 else POOL
                tt(e1, big[:, c, :], big[:, c, :], big[:, 4, :], ALU.subtract, [("big", c), ("big", 4)], [("big", c)])
                tt(e1, big[:, c, :], big[:, c, :], big[:, 6, :], ALU.mult, [("big", c), ("big", 6)], [("big", c)])
            for c in range(4):
                e1 = DVE if c % 2 == 0 else POOL
                act(big[:, c, :], big[:, c, :], AF.Silu, [("big", c), ("c", "pv")], [("big", c)],
                    bias=pv[:, l, PV_LNB + c:PV_LNB + c + 1], scale=pv[:, l, PV_LNG + c:PV_LNG + c + 1])
                tt(e1, cbo[:, c, :], big[:, c, :], cg[:, c, :], ALU.mult, [("big", c), ("cg", c)], [("cbo", c)])

            win_unit(6, lambda j, bk: act(ag[:, j, :], ps[bk][:], AF.Silu, [("ps", bk)], [("ag", j)]))
            win_unit(8, lambda j, bk: act(pg[:, j, :], ps[bk][:], AF.Silu, [("ps", bk)], [("pg", j)]))

            def pool_mix():
                for g in range(4):
                    bk = bank03()
                    mm(ps[bk][:], pw_bf[:, l, g, :], pmb[:, g, :], True, True, [("c", "pw"), ("pmb", g)], [("ps", bk)])
                    tmp, tmpn = [(rs2[:], ("rs2",)), (sg[0][:], ("sg", 0)), (sg[1][:], ("sg", 1)), (rs[:], ("rs",))][g]
                    ts(DVE, tmp, ps[bk][:], pv[:, l, PV_PB + g:PV_PB + g + 1], pv[:, l, PV_PS + g:PV_PS + g + 1],
                       ALU.add, ALU.mult, [("ps", bk), ("c", "pv")], [tmpn])
                    tt(POOL, pmb[:, g, :], tmp, pg[:, g, :], ALU.mult, [tmpn, ("pg", g)], [("pmb", g)])

            kcs = [kc for kc in range(-8, 8, 2) if (t > 0 or kc >= 0)]
            steps = []
            for i in range(4):
                for kc in kcs:
                    steps.append((2 * i, kc))
                    steps.append((2 * i + 1, kc))
            ns = len(steps)
            first_of = {}
            last_of = {}
            for j, (hd, kc) in enumerate(steps):
                first_of.setdefault(hd, j)
                last_of[hd] = j

            def geom(j):
                hd, kc = steps[j]
                cs, ce = max(0, kc), min(7, kc + 9)
                c0, c1 = cs * 64, (ce + 1) * 64
                j0 = (cs - kc) * 64
                if kc < 0:
                    sgm, koff, blk = 1 - seg, (8 + kc) * 64, (8 + kc) // 2
                else:
                    sgm, koff, blk = seg, kc * 64, kc // 2
                return hd, kc, c0, c1, j0, sgm, koff, blk

            SB6 = [0, 1, 2, 3, 6, 7]

            def emitS(j):
                hd, kc, c0, c1, j0, sgm, koff, blk = geom(j)
                i = hd // 2
                hh = hd % 2
                p0 = hh * 64
                n = c1 - c0
                bk = bank03()
                mm(ps[bk][:, 0:n], kT[l][p0:p0 + 64, i, sgm, koff:koff + 128], qT[p0:p0 + 64, i, c0:c1], True, True,
                   [("k", l, i, sgm), ("q", i, hh)], [("ps", bk)])
                tt(DVE, ps[bk][:, 0:n], ps[bk][:, 0:n], bias_sb[:, hd, j0:j0 + n], ALU.add,
                   [("ps", bk), ("bias",)], [("ps", bk)])
                act(PT[j % 4][:, 0:n], ps[bk][:, 0:n], AF.Exp, [("ps", bk)], [("PT", j % 4)])

            def emitPV(j):
                hd, kc, c0, c1, j0, sgm, koff, blk = geom(j)
                i = hd // 2
                hh = hd % 2
                n = c1 - c0
                first = first_of[hd] == j
                last = last_of[hd] == j
                ob = 4 + (i % 2) * 2 + hh
                rd = [("v", l, sgm, blk), ("PT", j % 4)]
                pt = PT[j % 4][:, 0:n]
                vl = Vb[l][:, sgm, blk, i, 0:128] if hh == 0 else Vb[l][:, sgm, blk, i, 64:192]
                mm(ps[ob][:, c0:c1], vl, pt, first, last, rd, [("ps", ob)], skip_group_check=True)
                if last:
                    pending.append((j + 1, lambda: normalize(hd, i, hh, ob)))

            def normalize(hd, i, hh, ob):
                o_lo, o_hi = (0, 64) if hh == 0 else (64, 128)
                d_lo, d_hi = (64, 128) if hh == 0 else (0, 64)
                cp(DVE, big[o_lo:o_hi, i, :], ps[ob][o_lo:o_hi, :], [("ps", ob)], [("big", i)])
                cp(DVE, big[o_lo:o_hi, 4 + i, :], ps[ob][d_lo:d_hi, :], [("ps", ob)], [("big", 4 + i)])

            def finish_attention():
                for i in range(4):
                    dn = big[:, 4 + i, :]
                    rD = [("big", 4 + i)]
                    act(dn, dn, AF.Ln, rD, rD)
                    act(dn, dn, AF.Exp, rD, rD, scale=-1.0)
                    tt(DVE, big[:, i, :], big[:, i, :], dn, ALU.mult, [("big", i)] + rD, [("big", i)])
                    tt(POOL, qT[:, i, :], big[:, i, :], ag[:, i, :], ALU.mult,
                       [("big", i), ("ag", i)], [("q", i, 0), ("q", i, 1)])

            pending = []
            emitS(0)
            emitS(1)
            for j in range(0, ns, 2):
                if j + 2 < ns:
                    emitS(j + 2)
                    emitS(j + 3)
                emitPV(j)
                emitPV(j + 1)
                for due, fn in [p for p in pending if p[0] <= j]:
                    fn()
                pending[:] = [p for p in pending if p[0] > j]
                if j == 6:
                    pool_mix()
            for due, fn in pending:
                fn()
            finish_attention()

            if DEBUG_STOP <= 4:
                if last_layer:
                    dma(SP, yT[s][:, t * T:(t + 1) * T].rearrange("(c p) t -> p c t", p=128), xb[:],
                        [X(c) for c in range(8)], [("y", ti)], ("st", xb_i))
                return
            br = {0: (cbo, lambda kc: [("cbo", kc)]),
                  1: (qT, lambda kc: [("q", kc, 0), ("q", kc, 1)]),
                  2: (pmb, lambda kc: [("pmb", kc)])}
            for mgi in range(2):
                for bi, b in enumerate(MERGE_ORDER):
                    u = 13 + (mgi * 3 + bi) * 2
                    sG = use_unit(l, u)
                    sB = use_unit(l, u + 1, extra=1)
                    bb, bregs = br[b]
                    hoist = last_layer and ti + 1 < len(tiles) and mgi == 1 and bi == 2
                    if hoist:
                        prenorm_sq(layers[0], ti + 1, hoisted=True)
                    for mi in range(4):
                        m = mgi * 4 + mi
                        bkG = bank()
                        for kc in range(8):
                            mm(ps[bkG][:], ring[sG][:, kc * 512 + mi * 128:kc * 512 + (mi + 1) * 128], h[:, kc, :],
                               kc == 0, kc == 7, [("ring", sG), ("h", kc)], [("ps", bkG)])
                        bkY = bank()
                        for kc in range(4):
                            mm(ps[bkY][:], ring[sB][:, kc * 512 + mi * 128:kc * 512 + (mi + 1) * 128], bb[:, kc, :],
                               kc == 0, kc == 3, [("ring", sB)] + bregs(kc), [("ps", bkY)])
                        si = state["sg"] % 2
                        state["sg"] += 1
                        act(sg[si][:], ps[bkG][:], AF.Sigmoid, [("ps", bkG)], [("sg", si)])
                        if bi == 0:
                            tt(DVE, big[:, mi, :], ps[bkY][:], sg[si][:], ALU.mult, [("ps", bkY), ("sg", si)],
                               [("big", mi)])
                        else:
                            tt(DVE, sg[si][:], ps[bkY][:], sg[si][:], ALU.mult, [("ps", bkY), ("sg", si)], [("sg", si)])
                            if bi == 1:
                                tt(POOL, big[:, mi, :], big[:, mi, :], sg[si][:], ALU.add, [("big", mi), ("sg", si)],
                                   [("big", mi)])
                            else:
                                tt(POOL, mrg[:, m, :], big[:, mi, :], sg[si][:], ALU.add, [("big", mi), ("sg", si)],
                                   [("mg", m)])
                    if hoist:
                        prenorm_fin(layers[0], ti + 1)

            if DEBUG_STOP <= 5:
                if last_layer:
                    dma(SP, yT[s][:, t * T:(t + 1) * T].rearrange("(c p) t -> p c t", p=128), xb[:],
                        [X(c) for c in range(8)], [("y", ti)], ("st", xb_i))
                return
            sq2 = [(cg[:, c, :], ("cg", c)) for c in range(4)] + [(ag[:, c, :], ("ag", c)) for c in range(4)]
            for hf in range(2):
                s_ = use_unit(l, 25 + hf)
                for mi in range(4):
                    m = hf * 4 + mi
                    bk = bank()
                    for kc in range(8):
                        mm(ps[bk][:], ring[s_][:, kc * 512 + mi * 128:kc * 512 + (mi + 1) * 128], mrg[:, kc, :],
                           kc == 0, kc == 7, [("ring", s_), ("mg", kc)], [("ps", bk)])
                    act(sq2[m][0], ps[bk][:], AF.Square, [("ps", bk)], [sq2[m][1]])
                    ts(DVE, big[:, m, :], ps[bk][:], pv[:, l, PV_POST + m:PV_POST + m + 1], None, ALU.mult, None,
                       [("ps", bk), ("c", "pv")], [("big", m)])
            if DEBUG_STOP <= 6:
                if last_layer:
                    dma(SP, yT[s][:, t * T:(t + 1) * T].rearrange("(c p) t -> p c t", p=128), xb[:],
                        [X(c) for c in range(8)], [("y", ti)], ("st", xb_i))
                return
            bk = bank()
            for c in range(8):
                mm(ps[bk][:], ones_bf[:], sq2[c][0], c == 0, c == 7, [("c", "ones"), sq2[c][1]], [("ps", bk)])
            act(rs[:], ps[bk][:], AF.Ln, [("ps", bk), ("c", "cst")], [("rs",)], bias=cst[:, 0:1], scale=1.0 / D)
            act(rs[:], rs[:], AF.Exp, [("rs",)], [("rs",)], scale=-0.5)
            for c in range(8):
                tt(DVE, big[:, c, :], big[:, c, :], rs[:], ALU.mult, [("big", c), ("rs",)], [("big", c)])
                tt(DVE, xb[:, c, :], xb[:, c, :], big[:, c, :], ALU.add, [X(c), ("big", c)], [X(c)])
            if last_layer:
                dma(SP, yT[s][:, t * T:(t + 1) * T].rearrange("(c p) t -> p c t", p=128), xb[:],
                    [X(c) for c in range(8)], [("y", ti)], ("st", xb_i))

        load_x(0)
        for ti, (s, t) in enumerate(tiles):
            if t == 0:
                for l in layers:
                    mset(POOL, uh[l][:], 0.0, [("uh", l)])
                    mset(POOL, ph[l][:], 0.0, [("ph", l)])
            for li, l in enumerate(layers):
                layer(l, ti, li == len(layers) - 1)
        fw = [("st", 0)] + ([("st", 1)] if len(tiles) > 1 else [])
        S.emit(nc, final_wait_streams=fw)
    return nc, S


def prep_shared(inp):
    f = lambda k: np.ascontiguousarray(np.asarray(inp[k], dtype=np.float32))

    def cols(v, n):
        return np.ascontiguousarray(v.reshape(L, n, 128).transpose(2, 0, 1))

    pvec = np.concatenate([
        cols(f("pre_norm_g"), 8), cols(f("post_norm_g"), 8), cols(f("conv_dw_b"), 4), cols(f("conv_ln_g"), 4),
        cols(f("conv_ln_b"), 4), cols(f("pool_b").reshape(L, 512), 4), cols(f("pool_scale"), 4)], axis=2)
    dw = f("conv_dw")
    dwT = np.ascontiguousarray(dw.reshape(L, 31, 4, 128).transpose(3, 0, 2, 1))
    rb = f("rel_bias")
    p = np.arange(128)[:, None]
    j = np.arange(640)[None, :]
    idx = np.clip(j - p, -256, 256) + 256
    biasT = np.ascontiguousarray(rb[:, :, idx].transpose(2, 0, 1, 3))
    ident = np.eye(128, dtype=np.float32)
    invcnt = np.ascontiguousarray(np.broadcast_to((1.0 / np.arange(1, 17, dtype=np.float32))[None, :], (128, 16)))
    return {
        "w_in": f("w_in"), "w_co": f("w_conv_out"), "w_ao": f("w_attn_out"), "w_po": f("w_pool_out"),
        "w_out": f("w_out"), "pool_w": f("pool_w"), "pvec": np.ascontiguousarray(pvec), "dwT": dwT,
        "biasT": biasT, "ident": ident, "invcnt": invcnt,
    }


_CACHE = {}


def run_model(inp, NSEQ, NT, layers=(0, 1)):
    x = np.asarray(inp["x"], dtype=np.float32)
    B, SL, _ = x.shape
    assert B == N_CORES * NSEQ and SL == NT * T
    key = (NSEQ, NT, tuple(layers))
    if key not in _CACHE:
        _CACHE[key] = build_program(NSEQ, NT, layers)[0]
    nc = _CACHE[key]
    shared = prep_shared(inp)
    in_maps = []
    for c in range(N_CORES):
        m = dict(shared)
        m["xT"] = np.ascontiguousarray(x[c * NSEQ:(c + 1) * NSEQ].transpose(0, 2, 1))
        in_maps.append(m)
    res = run_bass_kernel_spmd(nc, in_maps, core_ids=list(range(N_CORES)))
    out = np.empty((B, SL, D), dtype=np.float32)
    for c in range(N_CORES):
        out[c * NSEQ:(c + 1) * NSEQ] = res.results[c]["yT"].transpose(0, 2, 1)
    return out


def kernel(**inputs):
    return run_model(inputs, 4, 4, (0, 1))
```

```python
import contextlib
import numpy as np
import concourse.bass as bass
import concourse.mybir as mybir
from concourse.bass_utils import run_bass_kernel_spmd

F32 = mybir.dt.float32
BF16 = mybir.dt.bfloat16
AF = mybir.ActivationFunctionType
ALU = mybir.AluOpType

PE, ACT, DVE, POOL, SP = "pe", "act", "dve", "pool", "sp"
ENGINES = (PE, ACT, DVE, POOL, SP)
SEM_CAP = 30000

L = 2
D = 1024
T = 512
NU = 27
NPV = 36
NRING = 4
EPS = 1e-6
NEG = -30000.0
N_CORES = 8
DEBUG_STOP = 99


class _Op:
    __slots__ = ("eng", "fn", "deps", "dma", "stream", "idx", "sig", "tok")

    def __init__(self, eng, fn, deps, dma, stream, idx):
        self.eng = eng
        self.fn = fn
        self.deps = deps
        self.dma = dma
        self.stream = stream
        self.idx = idx
        self.sig = False
        self.tok = None


class Sched:
    def __init__(self):
        self.ops = []
        self.last_writer = {}
        self.readers = {}
        self.stream_concurrent = {}

    def add(self, eng, fn, reads=(), writes=(), dma=False, stream=None, concurrent=False):
        idx = len(self.ops)
        deps = set()
        for r in reads:
            w = self.last_writer.get(r)
            if w is not None:
                deps.add((w, "raw"))
        for r in writes:
            w = self.last_writer.get(r)
            if w is not None:
                deps.add((w, "waw"))
            for rd in self.readers.get(r, ()):
                deps.add((rd, "war"))
        for r in reads:
            if isinstance(r, tuple) and r[0] == "ps" and r not in writes:
                for rd in self.readers.get(r, ()):
                    deps.add((rd, "war"))
            self.readers.setdefault(r, []).append(idx)
        for r in writes:
            self.last_writer[r] = idx
            self.readers[r] = []
        if dma:
            self.stream_concurrent[stream] = concurrent or self.stream_concurrent.get(stream, False)
        self.ops.append(_Op(eng, fn, deps, dma, stream, idx))
        return idx

    def emit(self, nc, final_wait_streams=()):
        ops = self.ops
        need = {}
        for op in ops:
            lst = {}
            for (p, kind) in op.deps:
                prod = ops[p]
                if (not prod.dma) and (not op.dma) and prod.eng == op.eng:
                    if op.eng == POOL or (kind == "raw" and op.eng != PE):
                        lst[p] = True
                    continue
                lst[p] = True
            need[op.idx] = list(lst.keys())
            for p in lst:
                ops[p].sig = True
        eng_count = {e: 0 for e in ENGINES}
        stream_count = {}
        for op in ops:
            if op.dma:
                c = stream_count.get(op.stream, 0) + 16
                stream_count[op.stream] = c
                op.tok = (("dma", op.stream), c)
            elif op.sig:
                n = eng_count[op.eng]
                eng_count[op.eng] = n + 1
                op.tok = (("eng", op.eng, n // SEM_CAP), n % SEM_CAP + 1)
        for op in ops:
            if op.dma and self.stream_concurrent[op.stream]:
                op.tok = (op.tok[0], stream_count[op.stream])
        semkeys = []
        seen = set()
        for op in ops:
            if op.tok is not None and op.tok[0] not in seen:
                seen.add(op.tok[0])
                semkeys.append(op.tok[0])
        self.n_sems = len(semkeys)
        with contextlib.ExitStack() as st:
            sems = {}
            for i, k in enumerate(semkeys):
                sems[k] = st.enter_context(nc.semaphore("s%d" % i))
            block = st.enter_context(nc.Block())
            per_eng = {e: [] for e in ENGINES}
            for op in ops:
                per_eng[op.eng].append(op)

            def run_engine(e, handle):
                waited = {}
                for op in per_eng[e]:
                    w = {}
                    for p in need[op.idx]:
                        k, v = ops[p].tok
                        if v > w.get(k, 0):
                            w[k] = v
                    for k, v in w.items():
                        if waited.get(k, 0) >= v:
                            continue
                        handle.wait_ge(sems[k], v)
                        waited[k] = v
                    ins = op.fn(handle)
                    if op.tok is not None:
                        ins.then_inc(sems[op.tok[0]], 16 if op.dma else 1)
                if e == SP:
                    for s in final_wait_streams:
                        handle.wait_ge(sems[("dma", s)], stream_count[s])

            if per_eng[PE]:
                block.tensor(lambda hd: run_engine(PE, hd))
            if per_eng[ACT]:
                block.scalar(lambda hd: run_engine(ACT, hd))
            if per_eng[DVE]:
                block.vector(lambda hd: run_engine(DVE, hd))
            if per_eng[POOL]:
                block.gpsimd(lambda hd: run_engine(POOL, hd))
            block.sync(lambda hd: run_engine(SP, hd))


PV_PRE, PV_POST, PV_DWB, PV_LNG, PV_LNB, PV_PB, PV_PS = 0, 8, 16, 20, 24, 28, 32
WIN_UNITS = [512, 0, 1024, 1536, 2048, 2560, 3072, 3584, 4096]
MERGE_ORDER = (0, 2, 1)


def unit_cols(u):
    if 9 <= u <= 12:
        return 31 * 128
    if 13 <= u <= 24 and (u - 13) % 2 == 1:
        return 4 * 512
    return 4096


def build_program(NSEQ, NT, layers=(0, 1)):
    nc = bass.Bass("TRN2", target_bir_lowering=False)
    SL = NT * T

    def din(name, shape):
        return nc.dram_tensor(name, shape, F32, kind="ExternalInput").ap()

    xT = din("xT", [NSEQ, D, SL])
    w_in = din("w_in", [L, D, 7680])
    w_bo = [din("w_co", [L, 512, D]), din("w_ao", [L, 512, D]), din("w_po", [L, 512, D])]
    w_out = din("w_out", [L, D, D])
    pool_w = din("pool_w", [L, 4, 128, 128])
    pvec = din("pvec", [128, L, NPV])
    dwT = din("dwT", [128, L, 4, 31])
    biasT = din("biasT", [128, L, 8, 640])
    ident = din("ident", [128, 128])
    invcnt = din("invcnt", [128, 16])
    yT = nc.dram_tensor("yT", [NSEQ, D, SL], F32, kind="ExternalOutput").ap()
    wscr = nc.dram_tensor("wscr", [L, NU, 128, 4096], BF16).ap()
    bscr = nc.dram_tensor("bscr", [L, 128, 8 * 640], BF16).ap()

    S = Sched()
    with contextlib.ExitStack() as st:
        def sb(name, shape, dt):
            return st.enter_context(nc.sbuf_tensor("sb_" + name, shape, dt))

        xbuf = [sb("x%d" % i, [128, 8, T], F32) for i in range(2)]
        h = sb("h", [128, 8, T], BF16)
        mrg = sb("mrg", [128, 8, T], BF16)
        u_ext = sb("u", [128, 4, T + 30], BF16)
        cg = sb("cg", [128, 4, T], BF16)
        qT = sb("qT", [128, 4, T], BF16)
        kT = [sb("kT%d" % l, [128, 4, 2, T], BF16) for l in range(L)]
        Vb = [sb("V%d" % l, [128, 2, 4, 4, 192], BF16) for l in range(L)]
        ag = sb("ag", [128, 4, T], BF16)
        p_ext = sb("p", [128, 4, T + 15], F32)
        pg = sb("pg", [128, 4, T], BF16)
        big = sb("big", [128, 8, T], F32)
        cbo = sb("cbo", [128, 4, T], BF16)
        pmb = sb("pmb", [128, 4, T], BF16)
        PT = [sb("PT%d" % i, [128, T], BF16) for i in range(4)]
        rden = [sb("rden%d" % i, [128, T], F32) for i in range(2)]
        pa = sb("pa", [128, T + 15], F32)
        pb = sb("pb", [128, T + 15], F32)
        sg = [sb("sg%d" % i, [128, T], F32) for i in range(2)]
        rs = sb("rs", [128, T], F32)
        rs2 = sb("rs2", [128, T], F32)
        ring = [sb("ring%d" % i, [128, 4096], BF16) for i in range(NRING)]
        bias_sb = sb("bias", [128, 8, 640], BF16)
        ident_sb = sb("ident", [128, 128], F32)
        ones_bf = sb("ones", [128, 128], BF16)
        pv = sb("pv", [128, L, NPV], F32)
        dwT_sb = sb("dwT", [128, L, 4, 31], F32)
        pw_bf = sb("pw", [128, L, 4, 128], BF16)
        invc = sb("invc", [128, 16], F32)
        cst = sb("cst", [128, 4], F32)
        uh = [sb("uh%d" % l, [128, 4, 30], BF16) for l in range(L)]
        ph = [sb("ph%d" % l, [128, 4, 15], F32) for l in range(L)]
        ps = [st.enter_context(nc.psum_tensor("ps%d" % i, [128, T], F32)) for i in range(8)]

        def dma(eng, out, in_, reads, writes, stream, concurrent=False):
            S.add(eng, lambda e: e.dma_start(out=out, in_=in_), reads, writes, dma=True,
                  stream=stream, concurrent=concurrent)

        def mm(out, lhsT, rhs, start, stop, reads, writes, **kw):
            S.add(PE, lambda e: e.matmul(out, lhsT, rhs, start=start, stop=stop, **kw), reads, writes)

        def act(out, in_, func, reads, writes, bias=None, scale=None):
            kw = {}
            if bias is not None:
                kw["bias"] = bias
            if scale is not None:
                kw["scale"] = scale
            S.add(ACT, lambda e: e.activation(out=out, in_=in_, func=func, **kw), reads, writes)

        def tt(eng, out, in0, in1, op, reads, writes):
            S.add(eng, lambda e: e.tensor_tensor(out=out, in0=in0, in1=in1, op=op), reads, writes)

        def ts(eng, out, in0, s1, s2, op0, op1, reads, writes):
            if s2 is None:
                S.add(eng, lambda e: e.tensor_scalar(out=out, in0=in0, scalar1=s1, scalar2=None, op0=op0), reads, writes)
            else:
                S.add(eng, lambda e: e.tensor_scalar(out=out, in0=in0, scalar1=s1, scalar2=s2, op0=op0, op1=op1), reads, writes)

        def stt(eng, out, in0, scalar, in1, op0, op1, reads, writes):
            S.add(eng, lambda e: e.scalar_tensor_tensor(out=out, in0=in0, scalar=scalar, in1=in1, op0=op0, op1=op1),
                  reads, writes)

        def cp(eng, out, in_, reads, writes):
            S.add(eng, lambda e: e.tensor_copy(out=out, in_=in_), reads, writes)

        def mset(eng, ap, val, writes):
            S.add(eng, lambda e: e.memset(ap, val), (), writes)

        def recip(out, in_, reads, writes):
            S.add(DVE, lambda e: e.reciprocal(out=out, in_=in_), reads, writes)

        state = {"bk": 0, "sg": 0, "cursor": 0, "loaded": 0, "diag_j": 0, "prenormed": set(), "sbc": 0}

        def bank():
            state["bk"] = (state["bk"] + 1) % 8
            return state["bk"]

        def bank03():
            state["sbc"] += 1
            return state["sbc"] % 4

        def bank4():
            state["bk"] = (state["bk"] + 1) % 8
            if state["bk"] in (4, 5):
                state["bk"] = 6
            return state["bk"]

        S.add(SP, lambda e: e.dma_start(out=xbuf[0][:], in_=xT[0][:, 0:T].rearrange("(c p) t -> p c t", p=128)),
              [], [("x", 0, c) for c in range(8)], dma=True, stream=("xld", 0))
        dma(SP, ident_sb[:], ident, [], [("c", "ident")], "const", True)
        dma(SP, pv[:], pvec, [], [("c", "pv")], "const", True)
        dma(SP, dwT_sb[:], dwT, [], [("c", "dwT")], "const", True)
        dma(SP, invc[:], invcnt, [], [("c", "invc")], "const", True)
        mset(DVE, ones_bf[:], 1.0, [("c", "ones")])
        for l in layers:
            for sg_ in range(2):
                mset(POOL, Vb[l][:, sg_, :, :, 64:128], 1.0, [("vones", l)])
        mset(DVE, cst[:, 0:1], EPS, [("c", "cst")])
        mset(DVE, cst[:, 1:2], EPS, [("c", "cst")])
        def cast_unit(l, u, src, kc_n, grp):
            n = kc_n * 512
            dma(POOL, wscr[l, u][:, 0:n].rearrange("p (k c) -> p k c", k=kc_n),
                src.rearrange("(k p) c -> p k c", p=128), [], [("wscr", l, u)], ("cast", l, grp), True)

        def bias_setup():
            dma(POOL, pw_bf[:], pool_w.rearrange("l g c d -> c l g d"), [], [("c", "pw")], "pwld", True)
            for l in layers:
                dma(POOL, bias_sb[:], biasT[:, l], [], [("bias",)], "biasld")
                mset(POOL, bias_sb[64:128, :, 0:64], NEG, [("bias",)])
                mset(POOL, bias_sb[0:64, :, 576:640], NEG, [("bias",)])
                dma(POOL, bscr[l].rearrange("p (h j) -> p h j", h=8), bias_sb[:], [("bias",)], [("bscr", l)], "biasst")

        WGRP = {0: 0, 1: 1, 2: 1, 7: 2, 3: 2, 4: 3, 5: 3, 6: 4, 8: 4}
        for l in layers:
            for u in [0, 1, 2, 7, 3, 4, 5, 6, 8]:
                c0 = WIN_UNITS[u]
                cast_unit(l, u, w_in[l][:, c0:c0 + 512], 8, WGRP[u])
            if l == layers[0]:
                bias_setup()
            for mgi in range(2):
                for bi, b in enumerate(MERGE_ORDER):
                    u = 13 + (mgi * 3 + bi) * 2
                    c0 = 4608 + b * 1024 + mgi * 512
                    cast_unit(l, u, w_in[l][:, c0:c0 + 512], 8, 5 + mgi)
                    cast_unit(l, u + 1, w_bo[b][l][:, mgi * 512:(mgi + 1) * 512], 4, 5 + mgi)
            for hf in range(2):
                cast_unit(l, 25 + hf, w_out[l][:, hf * 512:(hf + 1) * 512], 8, 7)
        for l in layers:
            for c in range(4):
                s_ = state["diag_j"] % NRING
                state["diag_j"] += 1
                for k in range(31):
                    ts(DVE, ring[s_][:, k * 128:(k + 1) * 128], ident_sb[:], dwT_sb[:, l, c, k:k + 1], None,
                       ALU.mult, None, [("c", "ident"), ("c", "dwT")], [("ring", s_)])
                dma(SP, wscr[l, 9 + c][:, 0:3968], ring[s_][:, 0:3968], [("ring", s_)], [("wscr", l, 9 + c)],
                    ("diagst", s_))

        tiles = [(s, t) for s in range(NSEQ) for t in range(NT)]
        UORDER = [0, 1, 2, 9, 10, 11, 12, 7, 3, 4, 5, 6, 8] + list(range(13, NU))
        unit_seq = [(l, u) for _ in tiles for l in layers for u in UORDER]

        def emit_loads(upto):
            while state["loaded"] <= min(upto, len(unit_seq) - 1):
                i = state["loaded"]
                l, u = unit_seq[i]
                s_ = i % NRING
                n = unit_cols(u)
                dma(SP, ring[s_][:, 0:n], wscr[l, u][:, 0:n], [("wscr", l, u)], [("ring", s_)], ("ring", s_))
                state["loaded"] += 1

        def use_unit(l, u, extra=0):
            i = state["cursor"]
            assert unit_seq[i] == (l, u), (unit_seq[i], l, u)
            emit_loads(i + NRING - 1 - extra)
            state["cursor"] += 1
            return i % NRING

        def load_x(ti):
            s, t = tiles[ti]
            b = ti % 2
            dma(SP, xbuf[b][:], xT[s][:, t * T:(t + 1) * T].rearrange("(c p) t -> p c t", p=128),
                [], [("x", b, c) for c in range(8)], ("xld", b))

        def prenorm_sq(l, ti, hoisted=False):
            xb_i = ti % 2
            xb = xbuf[xb_i]
            X = lambda c: ("x", xb_i, c)
            sq = [(cbo[:, c, :], ("cbo", c)) for c in range(4)] + [(pmb[:, c, :], ("pmb", c)) for c in range(4)]
            for c in range(8):
                if c % 2 == 0 or not hoisted:
                    act(sq[c][0], xb[:, c, :], AF.Square, [X(c)], [sq[c][1]])
                else:
                    tt(DVE, sq[c][0], xb[:, c, :], xb[:, c, :], ALU.mult, [X(c)], [sq[c][1]])

        def prenorm_fin(l, ti):
            state["prenormed"].add((l, ti))
            xb_i = ti % 2
            xb = xbuf[xb_i]
            X = lambda c: ("x", xb_i, c)
            sq = [(cbo[:, c, :], ("cbo", c)) for c in range(4)] + [(pmb[:, c, :], ("pmb", c)) for c in range(4)]
            bk = bank03()
            for c in range(8):
                mm(ps[bk][:], ones_bf[:], sq[c][0], c == 0, c == 7, [("c", "ones"), sq[c][1]], [("ps", bk)])
            act(rs2[:], ps[bk][:], AF.Ln, [("ps", bk), ("c", "cst")], [("rs2",)], bias=cst[:, 0:1], scale=1.0 / D)
            act(rs2[:], rs2[:], AF.Exp, [("rs2",)], [("rs2",)], scale=-0.5)
            for c in range(8):
                stt(DVE, h[:, c, :], xb[:, c, :], pv[:, l, PV_PRE + c:PV_PRE + c + 1], rs2[:],
                    ALU.mult, ALU.mult, [X(c), ("c", "pv"), ("rs2",)], [("h", c)])

        def prenorm(l, ti, hoisted=False):
            prenorm_sq(l, ti, hoisted)
            prenorm_fin(l, ti)

        def layer(l, ti, last_layer):
            s, t = tiles[ti]
            xb_i = ti % 2
            xb = xbuf[xb_i]
            seg = t % 2
            X = lambda c: ("x", xb_i, c)

            def racc(out, in_, reads, writes, expo=-1.0, scale=None, bias=None):
                act(out, in_, AF.Ln, reads, writes, bias=bias, scale=scale)
                act(out, out, AF.Exp, writes, writes, scale=expo)

            dma(SP, bias_sb[:], bscr[l].rearrange("p (h j) -> p h j", h=8), [("bscr", l)], [("bias",)], "biasld2")

            if (l, ti) not in state["prenormed"]:
                prenorm(l, ti)
            if DEBUG_STOP <= 1:
                if last_layer:
                    dma(SP, yT[s][:, t * T:(t + 1) * T].rearrange("(c p) t -> p c t", p=128), xb[:],
                        [X(c) for c in range(8)], [("y", ti)], ("st", xb_i))
                return

            def win_unit(u, evac):
                s_ = use_unit(l, u)
                for j in range(4):
                    bk = bank()
                    for kc in range(8):
                        mm(ps[bk][:], ring[s_][:, kc * 512 + j * 128:kc * 512 + (j + 1) * 128], h[:, kc, :],
                           kc == 0, kc == 7, [("ring", s_), ("h", kc)], [("ps", bk)])
                    evac(j, bk)

            cp(POOL, u_ext[:, :, 0:30], uh[l][:], [("uh", l)], [("u", c) for c in range(4)])
            cp(POOL, p_ext[:, :, 0:15], ph[l][:], [("ph", l)], [("p", g) for g in range(4)])

            win_unit(0, lambda j, bk: act(pmb[:, j, :], ps[bk][:], AF.Sigmoid, [("ps", bk)], [("pmb", j)]))
            win_unit(1, lambda j, bk: tt(DVE, u_ext[:, j, 30:30 + T], ps[bk][:], pmb[:, j, :], ALU.mult,
                                         [("ps", bk), ("pmb", j)], [("u", j)]))
            win_unit(2, lambda j, bk: act(cg[:, j, :], ps[bk][:], AF.Silu, [("ps", bk)], [("cg", j)]))

            for c in range(4):
                s_ = use_unit(l, 9 + c)
                bk = bank()
                for k in range(31):
                    mm(ps[bk][:], ring[s_][:, k * 128:(k + 1) * 128], u_ext[:, c, k:k + T], k == 0, k == 30,
                       [("ring", s_), ("u", c)], [("ps", bk)])
                dwb = pv[:, l, PV_DWB + c:PV_DWB + c + 1]
                act(big[:, c, :], ps[bk][:], AF.Identity, [("ps", bk), ("c", "pv")], [("big", c)], bias=dwb)
                act(mrg[:, 4 + c, :], ps[bk][:], AF.Square, [("ps", bk), ("c", "pv")], [("mg", 4 + c)], bias=dwb)
                cp(POOL, mrg[:, c, :], big[:, c, :], [("big", c)], [("mg", c)])
            cp(POOL, uh[l][:], u_ext[:, :, T:T + 30], [("u", c) for c in range(4)], [("uh", l)])
            win_unit(7, lambda j, bk: act(p_ext[:, j, 15:15 + T], ps[bk][:], AF.Copy, [("ps", bk)], [("p", j)]))
            for g in range(4):
                w = 2 << g
                cur, oth = pa, pb
                tt(POOL, cur[:, 1:T + 15], p_ext[:, g, 1:T + 15], p_ext[:, g, 0:T + 14], ALU.add,
                   [("p", g)], [("pa",)])
                cur_n, oth_n = "pa", "pb"
                sh = 2
                while sh < w:
                    lo = 2 * sh - 1
                    tt(POOL, oth[:, lo:T + 15], cur[:, lo:T + 15], cur[:, lo - sh:T + 15 - sh], ALU.add,
                       [(cur_n,)], [(oth_n,)])
                    cur, oth = oth, cur
                    cur_n, oth_n = oth_n, cur_n
                    sh *= 2
                stt(DVE, pmb[:, g, :], cur[:, 15:15 + T], 1.0 / w, p_ext[:, g, 15:15 + T], ALU.mult, ALU.subtract,
                    [(cur_n,), ("p", g)], [("pmb", g)])
                if t == 0:
                    tt(POOL, oth[:, 0:w - 1], cur[:, 15:15 + w - 1], invc[:, 0:w - 1], ALU.mult,
                       [(cur_n,), ("c", "invc")], [(oth_n,)])
                    tt(POOL, pmb[:, g, 0:w - 1], oth[:, 0:w - 1], p_ext[:, g, 15:15 + w - 1], ALU.subtract,
                       [(oth_n,), ("p", g), ("pmb", g)], [("pmb", g)])
            cp(POOL, ph[l][:], p_ext[:, :, T:T + 15], [("p", g) for g in range(4)], [("ph", l)])

            if False:
                if last_layer:
                    dma(SP, yT[s][:, t * T:(t + 1) * T].rearrange("(c p) t -> p c t", p=128), xb[:],
                        [X(c) for c in range(8)], [("y", ti)], ("st", xb_i))
                return
            bk1 = bank()
            for c in range(4):
                mm(ps[bk1][:], ones_bf[:], mrg[:, c, :], c == 0, c == 3, [("c", "ones"), ("mg", c)], [("ps", bk1)])
            bk2 = bank()
            for c in range(4):
                mm(ps[bk2][:], ones_bf[:], mrg[:, 4 + c, :], c == 0, c == 3, [("c", "ones"), ("mg", 4 + c)], [("ps", bk2)])

            ts(DVE, big[:, 4, :], ps[bk1][:], 1.0 / 512, None, ALU.mult, None, [("ps", bk1)], [("big", 4)])
            ts(DVE, big[:, 5, :], ps[bk2][:], 1.0 / 512, None, ALU.mult, None, [("ps", bk2)], [("big", 5)])
            if DEBUG_STOP <= 2:
                if last_layer:
                    dma(SP, yT[s][:, t * T:(t + 1) * T].rearrange("(c p) t -> p c t", p=128), xb[:],
                        [X(c) for c in range(8)], [("y", ti)], ("st", xb_i))
                return
            if ti + 1 < len(tiles) and l == layers[0]:
                load_x(ti + 1)
            win_unit(3, lambda j, bk: act(qT[:, j, :], ps[bk][:], AF.Copy, [("ps", bk)], [("q", j, 0), ("q", j, 1)],
                                          scale=0.125))
            win_unit(4, lambda j, bk: act(kT[l][:, j, seg, :], ps[bk][:], AF.Copy, [("ps", bk)], [("k", l, j, seg)]))
            s_ = use_unit(l, 5)
            for tb in range(4):
                bk = bank()
                for kc in range(8):
                    mm(ps[bk][:], h[:, kc, tb * 128:(tb + 1) * 128], ring[s_][:, kc * 512:(kc + 1) * 512],
                       kc == 0, kc == 7, [("ring", s_), ("h", kc)], [("ps", bk)])
                pv4 = ps[bk][:].rearrange("p (i two d) -> p i two d", two=2, d=64)
                act(Vb[l][:, seg, tb, :, 0:64], pv4[:, :, 0, :], AF.Copy, [("ps", bk), ("vones", l)], [("v", l, seg, tb)])
                act(Vb[l][:, seg, tb, :, 128:192], pv4[:, :, 1, :], AF.Copy, [("ps", bk), ("vones", l)], [("v", l, seg, tb)])

            tt(DVE, big[:, 6, :], big[:, 4, :], big[:, 4, :], ALU.mult, [("big", 4)], [("big", 6)])
            stt(DVE, big[:, 5, :], big[:, 5, :], EPS, big[:, 6, :], ALU.add, ALU.subtract, [("big", 5), ("big", 6)], [("big", 5)])
            racc(big[:, 6, :], big[:, 5, :], [("big", 5)], [("big", 6)], expo=-0.5)
            for c in range(4):
                e1 = DVE if c % 2 == 0 else POOL
                tt(e1, big[:, c, :], big[:, c, :], big[:, 4, :], ALU.subtract, [("big", c), ("big", 4)], [("big", c)])
                tt(e1, big[:, c, :], big[:, c, :], big[:, 6, :], ALU.mult, [("big", c), ("big", 6)], [("big", c)])
            for c in range(4):
                e1 = DVE if c % 2 == 0 else POOL
                act(big[:, c, :], big[:, c, :], AF.Silu, [("big", c), ("c", "pv")], [("big", c)],
                    bias=pv[:, l, PV_LNB + c:PV_LNB + c + 1], scale=pv[:, l, PV_LNG + c:PV_LNG + c + 1])
                tt(e1, cbo[:, c, :], big[:, c, :], cg[:, c, :], ALU.mult, [("big", c), ("cg", c)], [("cbo", c)])

            win_unit(6, lambda j, bk: act(ag[:, j, :], ps[bk][:], AF.Silu, [("ps", bk)], [("ag", j)]))
            win_unit(8, lambda j, bk: act(pg[:, j, :], ps[bk][:], AF.Silu, [("ps", bk)], [("pg", j)]))

            def pool_mix():
                for g in range(4):
                    bk = bank03()
                    mm(ps[bk][:], pw_bf[:, l, g, :], pmb[:, g, :], True, True, [("c", "pw"), ("pmb", g)], [("ps", bk)])
                    tmp, tmpn = [(rs2[:], ("rs2",)), (sg[0][:], ("sg", 0)), (sg[1][:], ("sg", 1)), (rs[:], ("rs",))][g]
                    ts(DVE, tmp, ps[bk][:], pv[:, l, PV_PB + g:PV_PB + g + 1], pv[:, l, PV_PS + g:PV_PS + g + 1],
                       ALU.add, ALU.mult, [("ps", bk), ("c", "pv")], [tmpn])
                    tt(POOL, pmb[:, g, :], tmp, pg[:, g, :], ALU.mult, [tmpn, ("pg", g)], [("pmb", g)])

            kcs = [kc for kc in range(-8, 8, 2) if (t > 0 or kc >= 0)]
            steps = []
            for i in range(4):
                for kc in kcs:
                    steps.append((2 * i, kc))
                    steps.append((2 * i + 1, kc))
            ns = len(steps)
            first_of = {}
            last_of = {}
            for j, (hd, kc) in enumerate(steps):
                first_of.setdefault(hd, j)
                last_of[hd] = j

            def geom(j):
                hd, kc = steps[j]
                cs, ce = max(0, kc), min(7, kc + 9)
                c0, c1 = cs * 64, (ce + 1) * 64
                j0 = (cs - kc) * 64
                if kc < 0:
                    sgm, koff, blk = 1 - seg, (8 + kc) * 64, (8 + kc) // 2
                else:
                    sgm, koff, blk = seg, kc * 64, kc // 2
                return hd, kc, c0, c1, j0, sgm, koff, blk

            SB6 = [0, 1, 2, 3, 6, 7]

            def emitS(j):
                hd, kc, c0, c1, j0, sgm, koff, blk = geom(j)
                i = hd // 2
                hh = hd % 2
                p0 = hh * 64
                n = c1 - c0
                bk = bank03()
                mm(ps[bk][:, 0:n], kT[l][p0:p0 + 64, i, sgm, koff:koff + 128], qT[p0:p0 + 64, i, c0:c1], True, True,
                   [("k", l, i, sgm), ("q", i, hh)], [("ps", bk)])
                tt(DVE, ps[bk][:, 0:n], ps[bk][:, 0:n], bias_sb[:, hd, j0:j0 + n], ALU.add,
                   [("ps", bk), ("bias",)], [("ps", bk)])
                act(PT[j % 4][:, 0:n], ps[bk][:, 0:n], AF.Exp, [("ps", bk)], [("PT", j % 4)])

            def emitPV(j):
                hd, kc, c0, c1, j0, sgm, koff, blk = geom(j)
                i = hd // 2
                hh = hd % 2
                n = c1 - c0
                first = first_of[hd] == j
                last = last_of[hd] == j
                ob = 4 + (i % 2) * 2 + hh
                rd = [("v", l, sgm, blk), ("PT", j % 4)]
                pt = PT[j % 4][:, 0:n]
                vl = Vb[l][:, sgm, blk, i, 0:128] if hh == 0 else Vb[l][:, sgm, blk, i, 64:192]
                mm(ps[ob][:, c0:c1], vl, pt, first, last, rd, [("ps", ob)], skip_group_check=True)
                if last:
                    pending.append((j + 1, lambda: normalize(hd, i, hh, ob)))

            def normalize(hd, i, hh, ob):
                o_lo, o_hi = (0, 64) if hh == 0 else (64, 128)
                d_lo, d_hi = (64, 128) if hh == 0 else (0, 64)
                cp(DVE, big[o_lo:o_hi, i, :], ps[ob][o_lo:o_hi, :], [("ps", ob)], [("big", i)])
                cp(DVE, big[o_lo:o_hi, 4 + i, :], ps[ob][d_lo:d_hi, :], [("ps", ob)], [("big", 4 + i)])

            def finish_attention():
                for i in range(4):
                    dn = big[:, 4 + i, :]
                    rD = [("big", 4 + i)]
                    act(dn, dn, AF.Ln, rD, rD)
                    act(dn, dn, AF.Exp, rD, rD, scale=-1.0)
                    tt(DVE, big[:, i, :], big[:, i, :], dn, ALU.mult, [("big", i)] + rD, [("big", i)])
                    tt(POOL, qT[:, i, :], big[:, i, :], ag[:, i, :], ALU.mult,
                       [("big", i), ("ag", i)], [("q", i, 0), ("q", i, 1)])

            pending = []
            emitS(0)
            emitS(1)
            for j in range(0, ns, 2):
                if j + 2 < ns:
                    emitS(j + 2)
                    emitS(j + 3)
                emitPV(j)
                emitPV(j + 1)
                for due, fn in [p for p in pending if p[0] <= j]:
                    fn()
                pending[:] = [p for p in pending if p[0] > j]
                if j == 6:
                    pool_mix()
            for due, fn in pending:
                fn()
            finish_attention()

            if DEBUG_STOP <= 4:
                if last_layer:
                    dma(SP, yT[s][:, t * T:(t + 1) * T].rearrange("(c p) t -> p c t", p=128), xb[:],
                        [X(c) for c in range(8)], [("y", ti)], ("st", xb_i))
                return
            br = {0: (cbo, lambda kc: [("cbo", kc)]),
                  1: (qT, lambda kc: [("q", kc, 0), ("q", kc, 1)]),
                  2: (pmb, lambda kc: [("pmb", kc)])}
            for mgi in range(2):
                for bi, b in enumerate(MERGE_ORDER):
                    u = 13 + (mgi * 3 + bi) * 2
                    sG = use_unit(l, u)
                    sB = use_unit(l, u + 1, extra=1)
                    bb, bregs = br[b]
                    hoist = last_layer and ti + 1 < len(tiles) and mgi == 1 and bi == 2
                    if hoist:
                        prenorm_sq(layers[0], ti + 1, hoisted=True)
                    for mi in range(4):
                        m = mgi * 4 + mi
                        bkG = bank()
                        for kc in range(8):
                            mm(ps[bkG][:], ring[sG][:, kc * 512 + mi * 128:kc * 512 + (mi + 1) * 128], h[:, kc, :],
                               kc == 0, kc == 7, [("ring", sG), ("h", kc)], [("ps", bkG)])
                        bkY = bank()
                        for kc in range(4):
                            mm(ps[bkY][:], ring[sB][:, kc * 512 + mi * 128:kc * 512 + (mi + 1) * 128], bb[:, kc, :],
                               kc == 0, kc == 3, [("ring", sB)] + bregs(kc), [("ps", bkY)])
                        si = state["sg"] % 2
                        state["sg"] += 1
                        act(sg[si][:], ps[bkG][:], AF.Sigmoid, [("ps", bkG)], [("sg", si)])
                        if bi == 0:
                            tt(DVE, big[:, mi, :], ps[bkY][:], sg[si][:], ALU.mult, [("ps", bkY), ("sg", si)],
                               [("big", mi)])
                        else:
                            tt(DVE, sg[si][:], ps[bkY][:], sg[si][:], ALU.mult, [("ps", bkY), ("sg", si)], [("sg", si)])
                            if bi == 1:
                                tt(POOL, big[:, mi, :], big[:, mi, :], sg[si][:], ALU.add, [("big", mi), ("sg", si)],
                                   [("big", mi)])
                            else:
                                tt(POOL, mrg[:, m, :], big[:, mi, :], sg[si][:], ALU.add, [("big", mi), ("sg", si)],
                                   [("mg", m)])
                    if hoist:
                        prenorm_fin(layers[0], ti + 1)

            if DEBUG_STOP <= 5:
                if last_layer:
                    dma(SP, yT[s][:, t * T:(t + 1) * T].rearrange("(c p) t -> p c t", p=128), xb[:],
                        [X(c) for c in range(8)], [("y", ti)], ("st", xb_i))
                return
            sq2 = [(cg[:, c, :], ("cg", c)) for c in range(4)] + [(ag[:, c, :], ("ag", c)) for c in range(4)]
            for hf in range(2):
                s_ = use_unit(l, 25 + hf)
                for mi in range(4):
                    m = hf * 4 + mi
                    bk = bank()
                    for kc in range(8):
                        mm(ps[bk][:], ring[s_][:, kc * 512 + mi * 128:kc * 512 + (mi + 1) * 128], mrg[:, kc, :],
                           kc == 0, kc == 7, [("ring", s_), ("mg", kc)], [("ps", bk)])
                    act(sq2[m][0], ps[bk][:], AF.Square, [("ps", bk)], [sq2[m][1]])
                    ts(DVE, big[:, m, :], ps[bk][:], pv[:, l, PV_POST + m:PV_POST + m + 1], None, ALU.mult, None,
                       [("ps", bk), ("c", "pv")], [("big", m)])
            if DEBUG_STOP <= 6:
                if last_layer:
                    dma(SP, yT[s][:, t * T:(t + 1) * T].rearrange("(c p) t -> p c t", p=128), xb[:],
                        [X(c) for c in range(8)], [("y", ti)], ("st", xb_i))
                return
            bk = bank()
            for c in range(8):
                mm(ps[bk][:], ones_bf[:], sq2[c][0], c == 0, c == 7, [("c", "ones"), sq2[c][1]], [("ps", bk)])
            act(rs[:], ps[bk][:], AF.Ln, [("ps", bk), ("c", "cst")], [("rs",)], bias=cst[:, 0:1], scale=1.0 / D)
            act(rs[:], rs[:], AF.Exp, [("rs",)], [("rs",)], scale=-0.5)
            for c in range(8):
                tt(DVE, big[:, c, :], big[:, c, :], rs[:], ALU.mult, [("big", c), ("rs",)], [("big", c)])
                tt(DVE, xb[:, c, :], xb[:, c, :], big[:, c, :], ALU.add, [X(c), ("big", c)], [X(c)])
            if last_layer:
                dma(SP, yT[s][:, t * T:(t + 1) * T].rearrange("(c p) t -> p c t", p=128), xb[:],
                    [X(c) for c in range(8)], [("y", ti)], ("st", xb_i))

        for ti, (s, t) in enumerate(tiles):
            if t == 0:
                for l in layers:
                    mset(POOL, uh[l][:], 0.0, [("uh", l)])
                    mset(POOL, ph[l][:], 0.0, [("ph", l)])
            for li, l in enumerate(layers):
                layer(l, ti, li == len(layers) - 1)
        fw = [("st", 0)] + ([("st", 1)] if len(tiles) > 1 else [])
        S.emit(nc, final_wait_streams=fw)
    return nc, S


def prep_shared(inp):
    f = lambda k: np.ascontiguousarray(np.asarray(inp[k], dtype=np.float32))

    def cols(v, n):
        return np.ascontiguousarray(v.reshape(L, n, 128).transpose(2, 0, 1))

    pvec = np.concatenate([
        cols(f("pre_norm_g"), 8), cols(f("post_norm_g"), 8), cols(f("conv_dw_b"), 4), cols(f("conv_ln_g"), 4),
        cols(f("conv_ln_b"), 4), cols(f("pool_b").reshape(L, 512), 4), cols(f("pool_scale"), 4)], axis=2)
    dw = f("conv_dw")
    dwT = np.ascontiguousarray(dw.reshape(L, 31, 4, 128).transpose(3, 0, 2, 1))
    rb = f("rel_bias")
    p = np.arange(128)[:, None]
    j = np.arange(640)[None, :]
    idx = np.clip(j - p, -256, 256) + 256
    biasT = np.ascontiguousarray(rb[:, :, idx].transpose(2, 0, 1, 3))
    ident = np.eye(128, dtype=np.float32)
    invcnt = np.ascontiguousarray(np.broadcast_to((1.0 / np.arange(1, 17, dtype=np.float32))[None, :], (128, 16)))
    return {
        "w_in": f("w_in"), "w_co": f("w_conv_out"), "w_ao": f("w_attn_out"), "w_po": f("w_pool_out"),
        "w_out": f("w_out"), "pool_w": f("pool_w"), "pvec": np.ascontiguousarray(pvec), "dwT": dwT,
        "biasT": biasT, "ident": ident, "invcnt": invcnt,
    }


_CACHE = {}


def run_model(inp, NSEQ, NT, layers=(0, 1)):
    x = np.asarray(inp["x"], dtype=np.float32)
    B, SL, _ = x.shape
    assert B == N_CORES * NSEQ and SL == NT * T
    key = (NSEQ, NT, tuple(layers))
    if key not in _CACHE:
        _CACHE[key] = build_program(NSEQ, NT, layers)[0]
    nc = _CACHE[key]
    shared = prep_shared(inp)
    in_maps = []
    for c in range(N_CORES):
        m = dict(shared)
        m["xT"] = np.ascontiguousarray(x[c * NSEQ:(c + 1) * NSEQ].transpose(0, 2, 1))
        in_maps.append(m)
    res = run_bass_kernel_spmd(nc, in_maps, core_ids=list(range(N_CORES)))
    out = np.empty((B, SL, D), dtype=np.float32)
    for c in range(N_CORES):
        out[c * NSEQ:(c + 1) * NSEQ] = res.results[c]["yT"].transpose(0, 2, 1)
    return out


def kernel(**inputs):
    return run_model(inputs, 4, 4, (0, 1))
```

```python
import contextlib
import numpy as np
import concourse.bass as bass
import concourse.mybir as mybir
from concourse.bass_utils import run_bass_kernel_spmd

F32 = mybir.dt.float32
BF16 = mybir.dt.bfloat16
AF = mybir.ActivationFunctionType
ALU = mybir.AluOpType

PE, ACT, DVE, POOL, SP = "pe", "act", "dve", "pool", "sp"
ENGINES = (PE, ACT, DVE, POOL, SP)
SEM_CAP = 30000

L = 2
D = 1024
T = 512
NU = 27
NPV = 36
NRING = 4
EPS = 1e-6
NEG = -30000.0
N_CORES = 8
DEBUG_STOP = 99


class _Op:
    __slots__ = ("eng", "fn", "deps", "dma", "stream", "idx", "sig", "tok")

    def __init__(self, eng, fn, deps, dma, stream, idx):
        self.eng = eng
        self.fn = fn
        self.deps = deps
        self.dma = dma
        self.stream = stream
        self.idx = idx
        self.sig = False
        self.tok = None


class Sched:
    def __init__(self):
        self.ops = []
        self.last_writer = {}
        self.readers = {}
        self.stream_concurrent = {}

    def add(self, eng, fn, reads=(), writes=(), dma=False, stream=None, concurrent=False):
        idx = len(self.ops)
        deps = set()
        for r in reads:
            w = self.last_writer.get(r)
            if w is not None:
                deps.add((w, "raw"))
        for r in writes:
            w = self.last_writer.get(r)
            if w is not None:
                deps.add((w, "waw"))
            for rd in self.readers.get(r, ()):
                deps.add((rd, "war"))
        for r in reads:
            if isinstance(r, tuple) and r[0] == "ps" and r not in writes:
                for rd in self.readers.get(r, ()):
                    deps.add((rd, "war"))
            self.readers.setdefault(r, []).append(idx)
        for r in writes:
            self.last_writer[r] = idx
            self.readers[r] = []
        if dma:
            self.stream_concurrent[stream] = concurrent or self.stream_concurrent.get(stream, False)
        self.ops.append(_Op(eng, fn, deps, dma, stream, idx))
        return idx

    def emit(self, nc, final_wait_streams=()):
        ops = self.ops
        need = {}
        for op in ops:
            lst = {}
            for (p, kind) in op.deps:
                prod = ops[p]
                if (not prod.dma) and (not op.dma) and prod.eng == op.eng:
                    if op.eng == POOL or (kind == "raw" and op.eng != PE):
                        lst[p] = True
                    continue
                lst[p] = True
            need[op.idx] = list(lst.keys())
            for p in lst:
                ops[p].sig = True
        eng_count = {e: 0 for e in ENGINES}
        stream_count = {}
        for op in ops:
            if op.dma:
                c = stream_count.get(op.stream, 0) + 16
                stream_count[op.stream] = c
                op.tok = (("dma", op.stream), c)
            elif op.sig:
                n = eng_count[op.eng]
                eng_count[op.eng] = n + 1
                op.tok = (("eng", op.eng, n // SEM_CAP), n % SEM_CAP + 1)
        for op in ops:
            if op.dma and self.stream_concurrent[op.stream]:
                op.tok = (op.tok[0], stream_count[op.stream])
        semkeys = []
        seen = set()
        for op in ops:
            if op.tok is not None and op.tok[0] not in seen:
                seen.add(op.tok[0])
                semkeys.append(op.tok[0])
        self.n_sems = len(semkeys)
        with contextlib.ExitStack() as st:
            sems = {}
            for i, k in enumerate(semkeys):
                sems[k] = st.enter_context(nc.semaphore("s%d" % i))
            block = st.enter_context(nc.Block())
            per_eng = {e: [] for e in ENGINES}
            for op in ops:
                per_eng[op.eng].append(op)

            def run_engine(e, handle):
                waited = {}
                for op in per_eng[e]:
                    w = {}
                    for p in need[op.idx]:
                        k, v = ops[p].tok
                        if v > w.get(k, 0):
                            w[k] = v
                    for k, v in w.items():
                        if waited.get(k, 0) >= v:
                            continue
                        handle.wait_ge(sems[k], v)
                        waited[k] = v
                    ins = op.fn(handle)
                    if op.tok is not None:
                        ins.then_inc(sems[op.tok[0]], 16 if op.dma else 1)
                if e == SP:
                    for s in final_wait_streams:
                        handle.wait_ge(sems[("dma", s)], stream_count[s])

            if per_eng[PE]:
                block.tensor(lambda hd: run_engine(PE, hd))
            if per_eng[ACT]:
                block.scalar(lambda hd: run_engine(ACT, hd))
            if per_eng[DVE]:
                block.vector(lambda hd: run_engine(DVE, hd))
            if per_eng[POOL]:
                block.gpsimd(lambda hd: run_engine(POOL, hd))
            block.sync(lambda hd: run_engine(SP, hd))


PV_PRE, PV_POST, PV_DWB, PV_LNG, PV_LNB, PV_PB, PV_PS = 0, 8, 16, 20, 24, 28, 32
WIN_UNITS = [512, 0, 1024, 1536, 2048, 2560, 3072, 3584, 4096]
MERGE_ORDER = (0, 2, 1)


def unit_cols(u):
    if 9 <= u <= 12:
        return 31 * 128
    if 13 <= u <= 24 and (u - 13) % 2 == 1:
        return 4 * 512
    return 4096


def build_program(NSEQ, NT, layers=(0, 1)):
    nc = bass.Bass("TRN2", target_bir_lowering=False)
    SL = NT * T

    def din(name, shape):
        return nc.dram_tensor(name, shape, F32, kind="ExternalInput").ap()

    xT = din("xT", [NSEQ, D, SL])
    w_in = din("w_in", [L, D, 7680])
    w_bo = [din("w_co", [L, 512, D]), din("w_ao", [L, 512, D]), din("w_po", [L, 512, D])]
    w_out = din("w_out", [L, D, D])
    pool_w = din("pool_w", [L, 4, 128, 128])
    pvec = din("pvec", [128, L, NPV])
    dwT = din("dwT", [128, L, 4, 31])
    biasT = din("biasT", [128, L, 8, 640])
    ident = din("ident", [128, 128])
    invcnt = din("invcnt", [128, 16])
    yT = nc.dram_tensor("yT", [NSEQ, D, SL], F32, kind="ExternalOutput").ap()
    wscr = nc.dram_tensor("wscr", [L, NU, 128, 4096], BF16).ap()
    bscr = nc.dram_tensor("bscr", [L, 128, 8 * 640], BF16).ap()

    S = Sched()
    with contextlib.ExitStack() as st:
        def sb(name, shape, dt):
            return st.enter_context(nc.sbuf_tensor("sb_" + name, shape, dt))

        xbuf = [sb("x%d" % i, [128, 8, T], F32) for i in range(2)]
        h = sb("h", [128, 8, T], BF16)
        mrg = sb("mrg", [128, 8, T], BF16)
        u_ext = sb("u", [128, 4, T + 30], BF16)
        cg = sb("cg", [128, 4, T], BF16)
        qT = sb("qT", [128, 4, T], BF16)
        kT = [sb("kT%d" % l, [128, 4, 2, T], BF16) for l in range(L)]
        Vb = [sb("V%d" % l, [128, 2, 4, 4, 192], BF16) for l in range(L)]
        ag = sb("ag", [128, 4, T], BF16)
        p_ext = sb("p", [128, 4, T + 15], F32)
        pg = sb("pg", [128, 4, T], BF16)
        big = sb("big", [128, 8, T], F32)
        cbo = sb("cbo", [128, 4, T], BF16)
        pmb = sb("pmb", [128, 4, T], BF16)
        NPT = 8
        PT = [sb("PT%d" % i, [128, T], BF16) for i in range(NPT)]
        pa = sb("pa", [128, T + 15], F32)
        pb = sb("pb", [128, T + 15], F32)
        sg = [sb("sg%d" % i, [128, T], F32) for i in range(2)]
        rs = sb("rs", [128, T], F32)
        rs2 = sb("rs2", [128, T], F32)
        ring = [sb("ring%d" % i, [128, 4096], BF16) for i in range(NRING)]
        bias_sb = sb("bias", [128, 8, 640], BF16)
        ident_sb = sb("ident", [128, 128], F32)
        ones_bf = sb("ones", [128, 128], BF16)
        pv = sb("pv", [128, L, NPV], F32)
        dwT_sb = sb("dwT", [128, L, 4, 31], F32)
        pw_bf = sb("pw", [128, L, 4, 128], BF16)
        invc = sb("invc", [128, 16], F32)
        cst = sb("cst", [128, 4], F32)
        uh = [sb("uh%d" % l, [128, 4, 30], BF16) for l in range(L)]
        ph = [sb("ph%d" % l, [128, 4, 15], F32) for l in range(L)]
        ps = [st.enter_context(nc.psum_tensor("ps%d" % i, [128, T], F32)) for i in range(8)]

        def dma(eng, out, in_, reads, writes, stream, concurrent=False):
            S.add(eng, lambda e: e.dma_start(out=out, in_=in_), reads, writes, dma=True,
                  stream=stream, concurrent=concurrent)

        def mm(out, lhsT, rhs, start, stop, reads, writes, **kw):
            S.add(PE, lambda e: e.matmul(out, lhsT, rhs, start=start, stop=stop, **kw), reads, writes)

        def act(out, in_, func, reads, writes, bias=None, scale=None):
            kw = {}
            if bias is not None:
                kw["bias"] = bias
            if scale is not None:
                kw["scale"] = scale
            S.add(ACT, lambda e: e.activation(out=out, in_=in_, func=func, **kw), reads, writes)

        def tt(eng, out, in0, in1, op, reads, writes):
            S.add(eng, lambda e: e.tensor_tensor(out=out, in0=in0, in1=in1, op=op), reads, writes)

        def ts(eng, out, in0, s1, s2, op0, op1, reads, writes):
            if s2 is None:
                S.add(eng, lambda e: e.tensor_scalar(out=out, in0=in0, scalar1=s1, scalar2=None, op0=op0), reads, writes)
            else:
                S.add(eng, lambda e: e.tensor_scalar(out=out, in0=in0, scalar1=s1, scalar2=s2, op0=op0, op1=op1), reads, writes)

        def stt(eng, out, in0, scalar, in1, op0, op1, reads, writes):
            S.add(eng, lambda e: e.scalar_tensor_tensor(out=out, in0=in0, scalar=scalar, in1=in1, op0=op0, op1=op1),
                  reads, writes)

        def cp(eng, out, in_, reads, writes):
            S.add(eng, lambda e: e.tensor_copy(out=out, in_=in_), reads, writes)

        def mset(eng, ap, val, writes):
            S.add(eng, lambda e: e.memset(ap, val), (), writes)

        def recip(out, in_, reads, writes):
            S.add(DVE, lambda e: e.reciprocal(out=out, in_=in_), reads, writes)

        state = {"bk": 0, "sg": 0, "cursor": 0, "loaded": 0, "diag_j": 0, "prenormed": set(), "sbc": 0}

        def bank():
            state["bk"] = (state["bk"] + 1) % 8
            return state["bk"]

        def bank03():
            state["sbc"] += 1
            return state["sbc"] % 4

        def bank4():
            state["bk"] = (state["bk"] + 1) % 8
            if state["bk"] in (4, 5):
                state["bk"] = 6
            return state["bk"]

        dma(SP, ident_sb[:], ident, [], [("c", "ident")], "const", True)
        dma(SP, pv[:], pvec, [], [("c", "pv")], "const", True)
        dma(SP, dwT_sb[:], dwT, [], [("c", "dwT")], "const", True)
        dma(SP, invc[:], invcnt, [], [("c", "invc")], "const", True)
        mset(DVE, ones_bf[:], 1.0, [("c", "ones")])
        for l in layers:
            for sg_ in range(2):
                mset(POOL, Vb[l][:, sg_, :, :, 64:128], 1.0, [("vones", l)])
        mset(DVE, cst[:, 0:1], EPS, [("c", "cst")])
        mset(DVE, cst[:, 1:2], EPS, [("c", "cst")])
        def cast_unit(l, u, src, kc_n, grp):
            n = kc_n * 512
            dma(POOL, wscr[l, u][:, 0:n].rearrange("p (k c) -> p k c", k=kc_n),
                src.rearrange("(k p) c -> p k c", p=128), [], [("wscr", l, u)], ("cast", l, grp), True)

        def bias_setup():
            dma(POOL, pw_bf[:], pool_w.rearrange("l g c d -> c l g d"), [], [("c", "pw")], "pwld", True)
            for l in layers:
                dma(POOL, bias_sb[:], biasT[:, l], [], [("bias",)], "biasld")
                mset(POOL, bias_sb[64:128, :, 0:64], NEG, [("bias",)])
                mset(POOL, bias_sb[0:64, :, 576:640], NEG, [("bias",)])
                dma(SP, bscr[l].rearrange("p (h j) -> p h j", h=8), bias_sb[:], [("bias",)], [("bscr", l)], "biasst")

        WGRP = {0: 0, 1: 1, 2: 1, 7: 2, 3: 2, 4: 3, 5: 3, 6: 4, 8: 4}
        for l in layers:
            for u in [0, 1, 2, 7, 3, 4, 5, 6, 8]:
                c0 = WIN_UNITS[u]
                cast_unit(l, u, w_in[l][:, c0:c0 + 512], 8, WGRP[u])
            if l == layers[0]:
                bias_setup()
            for mgi in range(2):
                for bi, b in enumerate(MERGE_ORDER):
                    u = 13 + (mgi * 3 + bi) * 2
                    c0 = 4608 + b * 1024 + mgi * 512
                    cast_unit(l, u, w_in[l][:, c0:c0 + 512], 8, 5 + mgi)
                    cast_unit(l, u + 1, w_bo[b][l][:, mgi * 512:(mgi + 1) * 512], 4, 5 + mgi)
            for hf in range(2):
                cast_unit(l, 25 + hf, w_out[l][:, hf * 512:(hf + 1) * 512], 8, 7)
        for l in layers:
            for c in range(4):
                s_ = state["diag_j"] % NRING
                state["diag_j"] += 1
                for k in range(31):
                    ts(DVE, ring[s_][:, k * 128:(k + 1) * 128], ident_sb[:], dwT_sb[:, l, c, k:k + 1], None,
                       ALU.mult, None, [("c", "ident"), ("c", "dwT")], [("ring", s_)])
                dma(SP, wscr[l, 9 + c][:, 0:3968], ring[s_][:, 0:3968], [("ring", s_)], [("wscr", l, 9 + c)],
                    ("diagst", s_))

        tiles = [(s, t) for s in range(NSEQ) for t in range(NT)]
        UORDER = [0, 1, 2, 9, 10, 11, 12, 7, 3, 4, 5, 6, 8] + list(range(13, NU))
        unit_seq = [(l, u) for _ in tiles for l in layers for u in UORDER]

        def emit_loads(upto):
            while state["loaded"] <= min(upto, len(unit_seq) - 1):
                i = state["loaded"]
                l, u = unit_seq[i]
                s_ = i % NRING
                n = unit_cols(u)
                dma(SP, ring[s_][:, 0:n], wscr[l, u][:, 0:n], [("wscr", l, u)], [("ring", s_)], ("ring", s_))
                state["loaded"] += 1

        def use_unit(l, u, extra=0):
            i = state["cursor"]
            assert unit_seq[i] == (l, u), (unit_seq[i], l, u)
            emit_loads(i + NRING - 1 - extra)
            state["cursor"] += 1
            return i % NRING

        def load_x(ti):
            s, t = tiles[ti]
            b = ti % 2
            dma(SP, xbuf[b][:], xT[s][:, t * T:(t + 1) * T].rearrange("(c p) t -> p c t", p=128),
                [], [("x", b, c) for c in range(8)], ("xld", b))

        def prenorm_sq(l, ti, hoisted=False):
            xb_i = ti % 2
            xb = xbuf[xb_i]
            X = lambda c: ("x", xb_i, c)
            sq = [(cbo[:, c, :], ("cbo", c)) for c in range(4)] + [(pmb[:, c, :], ("pmb", c)) for c in range(4)]
            for c in range(8):
                if c % 2 == 0 or not hoisted:
                    act(sq[c][0], xb[:, c, :], AF.Square, [X(c)], [sq[c][1]])
                else:
                    tt(DVE, sq[c][0], xb[:, c, :], xb[:, c, :], ALU.mult, [X(c)], [sq[c][1]])

        def prenorm_fin(l, ti):
            state["prenormed"].add((l, ti))
            xb_i = ti % 2
            xb = xbuf[xb_i]
            X = lambda c: ("x", xb_i, c)
            sq = [(cbo[:, c, :], ("cbo", c)) for c in range(4)] + [(pmb[:, c, :], ("pmb", c)) for c in range(4)]
            bk = bank03()
            for c in range(8):
                mm(ps[bk][:], ones_bf[:], sq[c][0], c == 0, c == 7, [("c", "ones"), sq[c][1]], [("ps", bk)])
            act(rs2[:], ps[bk][:], AF.Ln, [("ps", bk), ("c", "cst")], [("rs2",)], bias=cst[:, 0:1], scale=1.0 / D)
            act(rs2[:], rs2[:], AF.Exp, [("rs2",)], [("rs2",)], scale=-0.5)
            for c in range(8):
                stt(DVE, h[:, c, :], xb[:, c, :], pv[:, l, PV_PRE + c:PV_PRE + c + 1], rs2[:],
                    ALU.mult, ALU.mult, [X(c), ("c", "pv"), ("rs2",)], [("h", c)])

        def prenorm(l, ti, hoisted=False):
            prenorm_sq(l, ti, hoisted)
            prenorm_fin(l, ti)

        def layer(l, ti, last_layer):
            s, t = tiles[ti]
            xb_i = ti % 2
            xb = xbuf[xb_i]
            seg = t % 2
            X = lambda c: ("x", xb_i, c)

            def racc(out, in_, reads, writes, expo=-1.0, scale=None, bias=None):
                act(out, in_, AF.Ln, reads, writes, bias=bias, scale=scale)
                act(out, out, AF.Exp, writes, writes, scale=expo)

            dma(SP, bias_sb[:], bscr[l].rearrange("p (h j) -> p h j", h=8), [("bscr", l)], [("bias",)], "biasld2")

            if (l, ti) not in state["prenormed"]:
                prenorm(l, ti)
            if DEBUG_STOP <= 1:
                if last_layer:
                    dma(SP, yT[s][:, t * T:(t + 1) * T].rearrange("(c p) t -> p c t", p=128), xb[:],
                        [X(c) for c in range(8)], [("y", ti)], ("st", xb_i))
                return

            def win_unit(u, evac):
                s_ = use_unit(l, u)
                for j in range(4):
                    bk = bank()
                    for kc in range(8):
                        mm(ps[bk][:], ring[s_][:, kc * 512 + j * 128:kc * 512 + (j + 1) * 128], h[:, kc, :],
                           kc == 0, kc == 7, [("ring", s_), ("h", kc)], [("ps", bk)])
                    evac(j, bk)

            cp(POOL, u_ext[:, :, 0:30], uh[l][:], [("uh", l)], [("u", c) for c in range(4)])
            cp(POOL, p_ext[:, :, 0:15], ph[l][:], [("ph", l)], [("p", g) for g in range(4)])

            win_unit(0, lambda j, bk: act(pmb[:, j, :], ps[bk][:], AF.Sigmoid, [("ps", bk)], [("pmb", j)]))
            win_unit(1, lambda j, bk: tt(DVE, u_ext[:, j, 30:30 + T], ps[bk][:], pmb[:, j, :], ALU.mult,
                                         [("ps", bk), ("pmb", j)], [("u", j)]))
            win_unit(2, lambda j, bk: act(cg[:, j, :], ps[bk][:], AF.Silu, [("ps", bk)], [("cg", j)]))

            for c in range(4):
                s_ = use_unit(l, 9 + c)
                bk = bank()
                for k in range(31):
                    mm(ps[bk][:], ring[s_][:, k * 128:(k + 1) * 128], u_ext[:, c, k:k + T], k == 0, k == 30,
                       [("ring", s_), ("u", c)], [("ps", bk)])
                dwb = pv[:, l, PV_DWB + c:PV_DWB + c + 1]
                act(big[:, c, :], ps[bk][:], AF.Identity, [("ps", bk), ("c", "pv")], [("big", c)], bias=dwb)
                act(mrg[:, 4 + c, :], ps[bk][:], AF.Square, [("ps", bk), ("c", "pv")], [("mg", 4 + c)], bias=dwb)
                cp(POOL, mrg[:, c, :], big[:, c, :], [("big", c)], [("mg", c)])
            cp(POOL, uh[l][:], u_ext[:, :, T:T + 30], [("u", c) for c in range(4)], [("uh", l)])
            win_unit(7, lambda j, bk: act(p_ext[:, j, 15:15 + T], ps[bk][:], AF.Copy, [("ps", bk)], [("p", j)]))
            for g in range(4):
                w = 2 << g
                cur, oth = pa, pb
                tt(POOL, cur[:, 1:T + 15], p_ext[:, g, 1:T + 15], p_ext[:, g, 0:T + 14], ALU.add,
                   [("p", g)], [("pa",)])
                cur_n, oth_n = "pa", "pb"
                sh = 2
                while sh < w:
                    lo = 2 * sh - 1
                    tt(POOL, oth[:, lo:T + 15], cur[:, lo:T + 15], cur[:, lo - sh:T + 15 - sh], ALU.add,
                       [(cur_n,)], [(oth_n,)])
                    cur, oth = oth, cur
                    cur_n, oth_n = oth_n, cur_n
                    sh *= 2
                stt(DVE, pmb[:, g, :], cur[:, 15:15 + T], 1.0 / w, p_ext[:, g, 15:15 + T], ALU.mult, ALU.subtract,
                    [(cur_n,), ("p", g)], [("pmb", g)])
                if t == 0:
                    tt(POOL, oth[:, 0:w - 1], cur[:, 15:15 + w - 1], invc[:, 0:w - 1], ALU.mult,
                       [(cur_n,), ("c", "invc")], [(oth_n,)])
                    tt(POOL, pmb[:, g, 0:w - 1], oth[:, 0:w - 1], p_ext[:, g, 15:15 + w - 1], ALU.subtract,
                       [(oth_n,), ("p", g), ("pmb", g)], [("pmb", g)])
            cp(POOL, ph[l][:], p_ext[:, :, T:T + 15], [("p", g) for g in range(4)], [("ph", l)])

            if False:
                if last_layer:
                    dma(SP, yT[s][:, t * T:(t + 1) * T].rearrange("(c p) t -> p c t", p=128), xb[:],
                        [X(c) for c in range(8)], [("y", ti)], ("st", xb_i))
                return
            bk1 = bank()
            for c in range(4):
                mm(ps[bk1][:], ones_bf[:], mrg[:, c, :], c == 0, c == 3, [("c", "ones"), ("mg", c)], [("ps", bk1)])
            bk2 = bank()
            for c in range(4):
                mm(ps[bk2][:], ones_bf[:], mrg[:, 4 + c, :], c == 0, c == 3, [("c", "ones"), ("mg", 4 + c)], [("ps", bk2)])

            ts(DVE, big[:, 4, :], ps[bk1][:], 1.0 / 512, None, ALU.mult, None, [("ps", bk1)], [("big", 4)])
            ts(DVE, big[:, 5, :], ps[bk2][:], 1.0 / 512, None, ALU.mult, None, [("ps", bk2)], [("big", 5)])
            if DEBUG_STOP <= 2:
                if last_layer:
                    dma(SP, yT[s][:, t * T:(t + 1) * T].rearrange("(c p) t -> p c t", p=128), xb[:],
                        [X(c) for c in range(8)], [("y", ti)], ("st", xb_i))
                return
            if ti + 1 < len(tiles) and l == layers[0]:
                load_x(ti + 1)
            win_unit(3, lambda j, bk: act(qT[:, j, :], ps[bk][:], AF.Copy, [("ps", bk)], [("q", j, 0), ("q", j, 1)],
                                          scale=0.125))
            win_unit(4, lambda j, bk: act(kT[l][:, j, seg, :], ps[bk][:], AF.Copy, [("ps", bk)], [("k", l, j, seg)]))
            s_ = use_unit(l, 5)
            for tb in range(4):
                bk = bank()
                for kc in range(8):
                    mm(ps[bk][:], h[:, kc, tb * 128:(tb + 1) * 128], ring[s_][:, kc * 512:(kc + 1) * 512],
                       kc == 0, kc == 7, [("ring", s_), ("h", kc)], [("ps", bk)])
                pv4 = ps[bk][:].rearrange("p (i two d) -> p i two d", two=2, d=64)
                act(Vb[l][:, seg, tb, :, 0:64], pv4[:, :, 0, :], AF.Copy, [("ps", bk), ("vones", l)], [("v", l, seg, tb)])
                act(Vb[l][:, seg, tb, :, 128:192], pv4[:, :, 1, :], AF.Copy, [("ps", bk), ("vones", l)], [("v", l, seg, tb)])

            tt(DVE, big[:, 6, :], big[:, 4, :], big[:, 4, :], ALU.mult, [("big", 4)], [("big", 6)])
            stt(DVE, big[:, 5, :], big[:, 5, :], EPS, big[:, 6, :], ALU.add, ALU.subtract, [("big", 5), ("big", 6)], [("big", 5)])
            racc(big[:, 6, :], big[:, 5, :], [("big", 5)], [("big", 6)], expo=-0.5)
            for c in range(4):
                e1 = DVE if c % 2 == 0 else POOL
                tt(e1, big[:, c, :], big[:, c, :], big[:, 4, :], ALU.subtract, [("big", c), ("big", 4)], [("big", c)])
                tt(e1, big[:, c, :], big[:, c, :], big[:, 6, :], ALU.mult, [("big", c), ("big", 6)], [("big", c)])
            for c in range(4):
                e1 = DVE if c % 2 == 0 else POOL
                act(big[:, c, :], big[:, c, :], AF.Silu, [("big", c), ("c", "pv")], [("big", c)],
                    bias=pv[:, l, PV_LNB + c:PV_LNB + c + 1], scale=pv[:, l, PV_LNG + c:PV_LNG + c + 1])
                tt(e1, cbo[:, c, :], big[:, c, :], cg[:, c, :], ALU.mult, [("big", c), ("cg", c)], [("cbo", c)])

            win_unit(6, lambda j, bk: act(ag[:, j, :], ps[bk][:], AF.Silu, [("ps", bk)], [("ag", j)]))
            win_unit(8, lambda j, bk: act(pg[:, j, :], ps[bk][:], AF.Silu, [("ps", bk)], [("pg", j)]))

            def pool_mix():
                for g in range(4):
                    bk = bank03()
                    mm(ps[bk][:], pw_bf[:, l, g, :], pmb[:, g, :], True, True, [("c", "pw"), ("pmb", g)], [("ps", bk)])
                    tmp, tmpn = [(rs2[:], ("rs2",)), (sg[0][:], ("sg", 0)), (sg[1][:], ("sg", 1)), (rs[:], ("rs",))][g]
                    ts(DVE, tmp, ps[bk][:], pv[:, l, PV_PB + g:PV_PB + g + 1], pv[:, l, PV_PS + g:PV_PS + g + 1],
                       ALU.add, ALU.mult, [("ps", bk), ("c", "pv")], [tmpn])
                    tt(POOL, pmb[:, g, :], tmp, pg[:, g, :], ALU.mult, [tmpn, ("pg", g)], [("pmb", g)])

            kcs = [kc for kc in range(-8, 8, 2) if (t > 0 or kc >= 0)]
            steps = []
            for i in range(4):
                for kc in kcs:
                    steps.append((2 * i, kc))
                    steps.append((2 * i + 1, kc))
            ns = len(steps)
            first_of = {}
            last_of = {}
            for j, (hd, kc) in enumerate(steps):
                first_of.setdefault(hd, j)
                last_of[hd] = j

            def geom(j):
                hd, kc = steps[j]
                cs, ce = max(0, kc), min(7, kc + 9)
                c0, c1 = cs * 64, (ce + 1) * 64
                j0 = (cs - kc) * 64
                if kc < 0:
                    sgm, koff, blk = 1 - seg, (8 + kc) * 64, (8 + kc) // 2
                else:
                    sgm, koff, blk = seg, kc * 64, kc // 2
                return hd, kc, c0, c1, j0, sgm, koff, blk

            SB6 = [0, 1, 2, 3, 6, 7]

            def emitS(j):
                hd, kc, c0, c1, j0, sgm, koff, blk = geom(j)
                i = hd // 2
                hh = hd % 2
                p0 = hh * 64
                n = c1 - c0
                bk = bank03()
                mm(ps[bk][:, 0:n], kT[l][p0:p0 + 64, i, sgm, koff:koff + 128], qT[p0:p0 + 64, i, c0:c1], True, True,
                   [("k", l, i, sgm), ("q", i, hh)], [("ps", bk)])
                tt(DVE, ps[bk][:, 0:n], ps[bk][:, 0:n], bias_sb[:, hd, j0:j0 + n], ALU.add,
                   [("ps", bk), ("bias",)], [("ps", bk)])
                act(PT[j % NPT][:, 0:n], ps[bk][:, 0:n], AF.Exp, [("ps", bk)], [("PT", j % NPT)])

            def emitPV(j):
                hd, kc, c0, c1, j0, sgm, koff, blk = geom(j)
                i = hd // 2
                hh = hd % 2
                n = c1 - c0
                first = first_of[hd] == j
                last = last_of[hd] == j
                ob = 4 + (i % 2) * 2 + hh
                rd = [("v", l, sgm, blk), ("PT", j % NPT)]
                pt = PT[j % NPT][:, 0:n]
                vl = Vb[l][:, sgm, blk, i, 0:128] if hh == 0 else Vb[l][:, sgm, blk, i, 64:192]
                mm(ps[ob][:, c0:c1], vl, pt, first, last, rd, [("ps", ob)], skip_group_check=True)
                if last:
                    pending.append((j + 1, lambda: normalize(hd, i, hh, ob)))

            def normalize(hd, i, hh, ob):
                o_lo, o_hi = (0, 64) if hh == 0 else (64, 128)
                d_lo, d_hi = (64, 128) if hh == 0 else (0, 64)
                cp(DVE, big[o_lo:o_hi, i, :], ps[ob][o_lo:o_hi, :], [("ps", ob)], [("big", i)])
                cp(DVE, big[o_lo:o_hi, 4 + i, :], ps[ob][d_lo:d_hi, :], [("ps", ob)], [("big", 4 + i)])

            def finish_attention():
                for i in range(4):
                    dn = big[:, 4 + i, :]
                    rD = [("big", 4 + i)]
                    act(dn, dn, AF.Ln, rD, rD)
                    act(dn, dn, AF.Exp, rD, rD, scale=-1.0)
                    tt(DVE, big[:, i, :], big[:, i, :], dn, ALU.mult, [("big", i)] + rD, [("big", i)])
                    tt(POOL, qT[:, i, :], big[:, i, :], ag[:, i, :], ALU.mult,
                       [("big", i), ("ag", i)], [("q", i, 0), ("q", i, 1)])

            pending = []
            for j0_ in range(0, min(4, ns)):
                emitS(j0_)
            for j in range(0, ns, 2):
                if j + 4 < ns:
                    emitS(j + 4)
                    emitS(j + 5)
                emitPV(j)
                emitPV(j + 1)
                for due, fn in [p for p in pending if p[0] <= j]:
                    fn()
                pending[:] = [p for p in pending if p[0] > j]
                if j == 6:
                    pool_mix()
            for due, fn in pending:
                fn()
            finish_attention()

            if DEBUG_STOP <= 4:
                if last_layer:
                    dma(SP, yT[s][:, t * T:(t + 1) * T].rearrange("(c p) t -> p c t", p=128), xb[:],
                        [X(c) for c in range(8)], [("y", ti)], ("st", xb_i))
                return
            br = {0: (cbo, lambda kc: [("cbo", kc)]),
                  1: (qT, lambda kc: [("q", kc, 0), ("q", kc, 1)]),
                  2: (pmb, lambda kc: [("pmb", kc)])}
            for mgi in range(2):
                for bi, b in enumerate(MERGE_ORDER):
                    u = 13 + (mgi * 3 + bi) * 2
                    sG = use_unit(l, u)
                    sB = use_unit(l, u + 1, extra=1)
                    bb, bregs = br[b]
                    hoist = last_layer and ti + 1 < len(tiles) and mgi == 1 and bi == 2
                    if hoist:
                        prenorm_sq(layers[0], ti + 1, hoisted=True)
                    for mi in range(4):
                        m = mgi * 4 + mi
                        bkG = bank()
                        for kc in range(8):
                            mm(ps[bkG][:], ring[sG][:, kc * 512 + mi * 128:kc * 512 + (mi + 1) * 128], h[:, kc, :],
                               kc == 0, kc == 7, [("ring", sG), ("h", kc)], [("ps", bkG)])
                        bkY = bank()
                        for kc in range(4):
                            mm(ps[bkY][:], ring[sB][:, kc * 512 + mi * 128:kc * 512 + (mi + 1) * 128], bb[:, kc, :],
                               kc == 0, kc == 3, [("ring", sB)] + bregs(kc), [("ps", bkY)])
                        si = state["sg"] % 2
                        state["sg"] += 1
                        act(sg[si][:], ps[bkG][:], AF.Sigmoid, [("ps", bkG)], [("sg", si)])
                        if bi == 0:
                            tt(DVE, big[:, mi, :], ps[bkY][:], sg[si][:], ALU.mult, [("ps", bkY), ("sg", si)],
                               [("big", mi)])
                        else:
                            tt(DVE, sg[si][:], ps[bkY][:], sg[si][:], ALU.mult, [("ps", bkY), ("sg", si)], [("sg", si)])
                            if bi == 1:
                                tt(POOL, big[:, mi, :], big[:, mi, :], sg[si][:], ALU.add, [("big", mi), ("sg", si)],
                                   [("big", mi)])
                            else:
                                tt(POOL, mrg[:, m, :], big[:, mi, :], sg[si][:], ALU.add, [("big", mi), ("sg", si)],
                                   [("mg", m)])
                    if hoist:
                        prenorm_fin(layers[0], ti + 1)

            if DEBUG_STOP <= 5:
                if last_layer:
                    dma(SP, yT[s][:, t * T:(t + 1) * T].rearrange("(c p) t -> p c t", p=128), xb[:],
                        [X(c) for c in range(8)], [("y", ti)], ("st", xb_i))
                return
            sq2 = [(cg[:, c, :], ("cg", c)) for c in range(4)] + [(ag[:, c, :], ("ag", c)) for c in range(4)]
            for hf in range(2):
                s_ = use_unit(l, 25 + hf)
                for mi in range(4):
                    m = hf * 4 + mi
                    bk = bank()
                    for kc in range(8):
                        mm(ps[bk][:], ring[s_][:, kc * 512 + mi * 128:kc * 512 + (mi + 1) * 128], mrg[:, kc, :],
                           kc == 0, kc == 7, [("ring", s_), ("mg", kc)], [("ps", bk)])
                    act(sq2[m][0], ps[bk][:], AF.Square, [("ps", bk)], [sq2[m][1]])
                    ts(DVE, big[:, m, :], ps[bk][:], pv[:, l, PV_POST + m:PV_POST + m + 1], None, ALU.mult, None,
                       [("ps", bk), ("c", "pv")], [("big", m)])
            if DEBUG_STOP <= 6:
                if last_layer:
                    dma(SP, yT[s][:, t * T:(t + 1) * T].rearrange("(c p) t -> p c t", p=128), xb[:],
                        [X(c) for c in range(8)], [("y", ti)], ("st", xb_i))
                return
            bk = bank()
            for c in range(8):
                mm(ps[bk][:], ones_bf[:], sq2[c][0], c == 0, c == 7, [("c", "ones"), sq2[c][1]], [("ps", bk)])
            act(rs[:], ps[bk][:], AF.Ln, [("ps", bk), ("c", "cst")], [("rs",)], bias=cst[:, 0:1], scale=1.0 / D)
            act(rs[:], rs[:], AF.Exp, [("rs",)], [("rs",)], scale=-0.5)
            for c in range(8):
                tt(DVE, big[:, c, :], big[:, c, :], rs[:], ALU.mult, [("big", c), ("rs",)], [("big", c)])
                tt(DVE, xb[:, c, :], xb[:, c, :], big[:, c, :], ALU.add, [X(c), ("big", c)], [X(c)])
            if last_layer:
                dma(SP, yT[s][:, t * T:(t + 1) * T].rearrange("(c p) t -> p c t", p=128), xb[:],
                    [X(c) for c in range(8)], [("y", ti)], ("st", xb_i))

        load_x(0)
        for ti, (s, t) in enumerate(tiles):
            if t == 0:
                for l in layers:
                    mset(POOL, uh[l][:], 0.0, [("uh", l)])
                    mset(POOL, ph[l][:], 0.0, [("ph", l)])
            for li, l in enumerate(layers):
                layer(l, ti, li == len(layers) - 1)
        fw = [("st", 0)] + ([("st", 1)] if len(tiles) > 1 else [])
        S.emit(nc, final_wait_streams=fw)
    return nc, S


def prep_shared(inp):
    f = lambda k: np.ascontiguousarray(np.asarray(inp[k], dtype=np.float32))

    def cols(v, n):
        return np.ascontiguousarray(v.reshape(L, n, 128).transpose(2, 0, 1))

    pvec = np.concatenate([
        cols(f("pre_norm_g"), 8), cols(f("post_norm_g"), 8), cols(f("conv_dw_b"), 4), cols(f("conv_ln_g"), 4),
        cols(f("conv_ln_b"), 4), cols(f("pool_b").reshape(L, 512), 4), cols(f("pool_scale"), 4)], axis=2)
    dw = f("conv_dw")
    dwT = np.ascontiguousarray(dw.reshape(L, 31, 4, 128).transpose(3, 0, 2, 1))
    rb = f("rel_bias")
    p = np.arange(128)[:, None]
    j = np.arange(640)[None, :]
    idx = np.clip(j - p, -256, 256) + 256
    biasT = np.ascontiguousarray(rb[:, :, idx].transpose(2, 0, 1, 3))
    ident = np.eye(128, dtype=np.float32)
    invcnt = np.ascontiguousarray(np.broadcast_to((1.0 / np.arange(1, 17, dtype=np.float32))[None, :], (128, 16)))
    return {
        "w_in": f("w_in"), "w_co": f("w_conv_out"), "w_ao": f("w_attn_out"), "w_po": f("w_pool_out"),
        "w_out": f("w_out"), "pool_w": f("pool_w"), "pvec": np.ascontiguousarray(pvec), "dwT": dwT,
        "biasT": biasT, "ident": ident, "invcnt": invcnt,
    }


_CACHE = {}


def run_model(inp, NSEQ, NT, layers=(0, 1)):
    x = np.asarray(inp["x"], dtype=np.float32)
    B, SL, _ = x.shape
    assert B == N_CORES * NSEQ and SL == NT * T
    key = (NSEQ, NT, tuple(layers))
    if key not in _CACHE:
        _CACHE[key] = build_program(NSEQ, NT, layers)[0]
    nc = _CACHE[key]
    shared = prep_shared(inp)
    in_maps = []
    for c in range(N_CORES):
        m = dict(shared)
        m["xT"] = np.ascontiguousarray(x[c * NSEQ:(c + 1) * NSEQ].transpose(0, 2, 1))
        in_maps.append(m)
    res = run_bass_kernel_spmd(nc, in_maps, core_ids=list(range(N_CORES)))
    out = np.empty((B, SL, D), dtype=np.float32)
    for c in range(N_CORES):
        out[c * NSEQ:(c + 1) * NSEQ] = res.results[c]["yT"].transpose(0, 2, 1)
    return out


def kernel(**inputs):
    return run_model(inputs, 4, 4, (0, 1))
```

```python
import contextlib
import numpy as np
import concourse.bass as bass
import concourse.mybir as mybir
from concourse.bass_utils import run_bass_kernel_spmd

F32 = mybir.dt.float32
BF16 = mybir.dt.bfloat16
AF = mybir.ActivationFunctionType
ALU = mybir.AluOpType

PE, ACT, DVE, POOL, SP = "pe", "act", "dve", "pool", "sp"
ENGINES = (PE, ACT, DVE, POOL, SP)
SEM_CAP = 30000

L = 2
D = 1024
T = 512
NU = 27
NPV = 36
NRING = 4
EPS = 1e-6
NEG = -30000.0
N_CORES = 8
DEBUG_STOP = 99


class _Op:
    __slots__ = ("eng", "fn", "deps", "dma", "stream", "idx", "sig", "tok")

    def __init__(self, eng, fn, deps, dma, stream, idx):
        self.eng = eng
        self.fn = fn
        self.deps = deps
        self.dma = dma
        self.stream = stream
        self.idx = idx
        self.sig = False
        self.tok = None


class Sched:
    def __init__(self):
        self.ops = []
        self.last_writer = {}
        self.readers = {}
        self.stream_concurrent = {}

    def add(self, eng, fn, reads=(), writes=(), dma=False, stream=None, concurrent=False):
        idx = len(self.ops)
        deps = set()
        for r in reads:
            w = self.last_writer.get(r)
            if w is not None:
                deps.add((w, "raw"))
        for r in writes:
            w = self.last_writer.get(r)
            if w is not None:
                deps.add((w, "waw"))
            for rd in self.readers.get(r, ()):
                deps.add((rd, "war"))
        for r in reads:
            if isinstance(r, tuple) and r[0] == "ps" and r not in writes:
                for rd in self.readers.get(r, ()):
                    deps.add((rd, "war"))
            self.readers.setdefault(r, []).append(idx)
        for r in writes:
            self.last_writer[r] = idx
            self.readers[r] = []
        if dma:
            self.stream_concurrent[stream] = concurrent or self.stream_concurrent.get(stream, False)
        self.ops.append(_Op(eng, fn, deps, dma, stream, idx))
        return idx

    def emit(self, nc, final_wait_streams=()):
        ops = self.ops
        need = {}
        for op in ops:
            lst = {}
            for (p, kind) in op.deps:
                prod = ops[p]
                if (not prod.dma) and (not op.dma) and prod.eng == op.eng:
                    if op.eng == POOL or (kind == "raw" and op.eng != PE):
                        lst[p] = True
                    continue
                lst[p] = True
            need[op.idx] = list(lst.keys())
            for p in lst:
                ops[p].sig = True
        eng_count = {e: 0 for e in ENGINES}
        stream_count = {}
        for op in ops:
            if op.dma:
                c = stream_count.get(op.stream, 0) + 16
                stream_count[op.stream] = c
                op.tok = (("dma", op.stream), c)
            elif op.sig:
                n = eng_count[op.eng]
                eng_count[op.eng] = n + 1
                op.tok = (("eng", op.eng, n // SEM_CAP), n % SEM_CAP + 1)
        for op in ops:
            if op.dma and self.stream_concurrent[op.stream]:
                op.tok = (op.tok[0], stream_count[op.stream])
        semkeys = []
        seen = set()
        for op in ops:
            if op.tok is not None and op.tok[0] not in seen:
                seen.add(op.tok[0])
                semkeys.append(op.tok[0])
        self.n_sems = len(semkeys)
        with contextlib.ExitStack() as st:
            sems = {}
            for i, k in enumerate(semkeys):
                sems[k] = st.enter_context(nc.semaphore("s%d" % i))
            block = st.enter_context(nc.Block())
            per_eng = {e: [] for e in ENGINES}
            for op in ops:
                per_eng[op.eng].append(op)

            def run_engine(e, handle):
                waited = {}
                for op in per_eng[e]:
                    w = {}
                    for p in need[op.idx]:
                        k, v = ops[p].tok
                        if v > w.get(k, 0):
                            w[k] = v
                    for k, v in w.items():
                        if waited.get(k, 0) >= v:
                            continue
                        handle.wait_ge(sems[k], v)
                        waited[k] = v
                    ins = op.fn(handle)
                    if op.tok is not None:
                        ins.then_inc(sems[op.tok[0]], 16 if op.dma else 1)
                if e == SP:
                    for s in final_wait_streams:
                        handle.wait_ge(sems[("dma", s)], stream_count[s])

            if per_eng[PE]:
                block.tensor(lambda hd: run_engine(PE, hd))
            if per_eng[ACT]:
                block.scalar(lambda hd: run_engine(ACT, hd))
            if per_eng[DVE]:
                block.vector(lambda hd: run_engine(DVE, hd))
            if per_eng[POOL]:
                block.gpsimd(lambda hd: run_engine(POOL, hd))
            block.sync(lambda hd: run_engine(SP, hd))


PV_PRE, PV_POST, PV_DWB, PV_LNG, PV_LNB, PV_PB, PV_PS = 0, 8, 16, 20, 24, 28, 32
WIN_UNITS = [512, 0, 1024, 1536, 2048, 2560, 3072, 3584, 4096]
MERGE_ORDER = (0, 2, 1)


def unit_cols(u):
    if 9 <= u <= 12:
        return 31 * 128
    if 13 <= u <= 24 and (u - 13) % 2 == 1:
        return 4 * 512
    return 4096


def build_program(NSEQ, NT, layers=(0, 1)):
    nc = bass.Bass("TRN2", target_bir_lowering=False)
    SL = NT * T

    def din(name, shape):
        return nc.dram_tensor(name, shape, F32, kind="ExternalInput").ap()

    xT = din("xT", [NSEQ, D, SL])
    w_in = din("w_in", [L, D, 7680])
    w_bo = [din("w_co", [L, 512, D]), din("w_ao", [L, 512, D]), din("w_po", [L, 512, D])]
    w_out = din("w_out", [L, D, D])
    pool_w = din("pool_w", [L, 4, 128, 128])
    pvec = din("pvec", [128, L, NPV])
    dwT = din("dwT", [128, L, 4, 31])
    biasT = din("biasT", [128, L, 8, 640])
    ident = din("ident", [128, 128])
    invcnt = din("invcnt", [128, 16])
    yT = nc.dram_tensor("yT", [NSEQ, D, SL], F32, kind="ExternalOutput").ap()
    wscr = nc.dram_tensor("wscr", [L, NU, 128, 4096], BF16).ap()
    bscr = nc.dram_tensor("bscr", [L, 128, 8 * 640], BF16).ap()

    S = Sched()
    with contextlib.ExitStack() as st:
        def sb(name, shape, dt):
            return st.enter_context(nc.sbuf_tensor("sb_" + name, shape, dt))

        xbuf = [sb("x%d" % i, [128, 8, T], F32) for i in range(2)]
        h = sb("h", [128, 8, T], BF16)
        mrg = sb("mrg", [128, 8, T], BF16)
        u_ext = sb("u", [128, 4, T + 30], BF16)
        cg = sb("cg", [128, 4, T], BF16)
        qT = sb("qT", [128, 4, T], BF16)
        kT = [sb("kT%d" % l, [128, 4, 2, T], BF16) for l in range(L)]
        Vb = [sb("V%d" % l, [128, 2, 4, 4, 192], BF16) for l in range(L)]
        ag = sb("ag", [128, 4, T], BF16)
        p_ext = sb("p", [128, 4, T + 15], F32)
        pg = sb("pg", [128, 4, T], BF16)
        big = sb("big", [128, 8, T], F32)
        cbo = sb("cbo", [128, 4, T], BF16)
        pmb = sb("pmb", [128, 4, T], BF16)
        NPT = 8
        PT = [sb("PT%d" % i, [128, T], BF16) for i in range(NPT)]
        pa = sb("pa", [128, T + 15], F32)
        pb = sb("pb", [128, T + 15], F32)
        sg = [sb("sg%d" % i, [128, T], F32) for i in range(2)]
        rs = sb("rs", [128, T], F32)
        rs2 = sb("rs2", [128, T], F32)
        ring = [sb("ring%d" % i, [128, 4096], BF16) for i in range(NRING)]
        bias_sb = sb("bias", [128, 8, 640], BF16)
        ident_sb = sb("ident", [128, 128], F32)
        ones_bf = sb("ones", [128, 128], BF16)
        pv = sb("pv", [128, L, NPV], F32)
        dwT_sb = sb("dwT", [128, L, 4, 31], F32)
        pw_bf = sb("pw", [128, L, 4, 128], BF16)
        invc = sb("invc", [128, 16], F32)
        cst = sb("cst", [128, 4], F32)
        uh = [sb("uh%d" % l, [128, 4, 30], BF16) for l in range(L)]
        ph = [sb("ph%d" % l, [128, 4, 15], F32) for l in range(L)]
        ps = [st.enter_context(nc.psum_tensor("ps%d" % i, [128, T], F32)) for i in range(8)]

        def dma(eng, out, in_, reads, writes, stream, concurrent=False):
            S.add(eng, lambda e: e.dma_start(out=out, in_=in_), reads, writes, dma=True,
                  stream=stream, concurrent=concurrent)

        def mm(out, lhsT, rhs, start, stop, reads, writes, **kw):
            S.add(PE, lambda e: e.matmul(out, lhsT, rhs, start=start, stop=stop, **kw), reads, writes)

        def act(out, in_, func, reads, writes, bias=None, scale=None):
            kw = {}
            if bias is not None:
                kw["bias"] = bias
            if scale is not None:
                kw["scale"] = scale
            S.add(ACT, lambda e: e.activation(out=out, in_=in_, func=func, **kw), reads, writes)

        def tt(eng, out, in0, in1, op, reads, writes):
            S.add(eng, lambda e: e.tensor_tensor(out=out, in0=in0, in1=in1, op=op), reads, writes)

        def ts(eng, out, in0, s1, s2, op0, op1, reads, writes):
            if s2 is None:
                S.add(eng, lambda e: e.tensor_scalar(out=out, in0=in0, scalar1=s1, scalar2=None, op0=op0), reads, writes)
            else:
                S.add(eng, lambda e: e.tensor_scalar(out=out, in0=in0, scalar1=s1, scalar2=s2, op0=op0, op1=op1), reads, writes)

        def stt(eng, out, in0, scalar, in1, op0, op1, reads, writes):
            S.add(eng, lambda e: e.scalar_tensor_tensor(out=out, in0=in0, scalar=scalar, in1=in1, op0=op0, op1=op1),
                  reads, writes)

        def cp(eng, out, in_, reads, writes):
            S.add(eng, lambda e: e.tensor_copy(out=out, in_=in_), reads, writes)

        def mset(eng, ap, val, writes):
            S.add(eng, lambda e: e.memset(ap, val), (), writes)

        def recip(out, in_, reads, writes):
            S.add(DVE, lambda e: e.reciprocal(out=out, in_=in_), reads, writes)

        state = {"bk": 0, "sg": 0, "cursor": 0, "loaded": 0, "diag_j": 0, "prenormed": set(), "sbc": 0}

        def bank():
            state["bk"] = (state["bk"] + 1) % 8
            return state["bk"]

        def bank03():
            state["sbc"] += 1
            return state["sbc"] % 4

        def bank4():
            state["bk"] = (state["bk"] + 1) % 8
            if state["bk"] in (4, 5):
                state["bk"] = 6
            return state["bk"]

        dma(SP, ident_sb[:], ident, [], [("c", "ident")], "const", True)
        dma(SP, pv[:], pvec, [], [("c", "pv")], "const", True)
        dma(SP, dwT_sb[:], dwT, [], [("c", "dwT")], "const", True)
        dma(SP, invc[:], invcnt, [], [("c", "invc")], "const", True)
        mset(DVE, ones_bf[:], 1.0, [("c", "ones")])
        for l in layers:
            for sg_ in range(2):
                mset(POOL, Vb[l][:, sg_, :, :, 64:128], 1.0, [("vones", l)])
        mset(DVE, cst[:, 0:1], EPS, [("c", "cst")])
        mset(DVE, cst[:, 1:2], EPS, [("c", "cst")])
        def cast_unit(l, u, src, kc_n, grp):
            n = kc_n * 512
            dma(POOL, wscr[l, u][:, 0:n].rearrange("p (k c) -> p k c", k=kc_n),
                src.rearrange("(k p) c -> p k c", p=128), [], [("wscr", l, u)], ("cast", l, grp), True)

        def bias_setup():
            dma(POOL, pw_bf[:], pool_w.rearrange("l g c d -> c l g d"), [], [("c", "pw")], "pwld", True)
            for l in layers:
                dma(POOL, bias_sb[:], biasT[:, l], [], [("bias",)], "biasld")
                mset(POOL, bias_sb[64:128, :, 0:64], NEG, [("bias",)])
                mset(POOL, bias_sb[0:64, :, 576:640], NEG, [("bias",)])
                dma(SP, bscr[l].rearrange("p (h j) -> p h j", h=8), bias_sb[:], [("bias",)], [("bscr", l)], "biasst")

        WGRP = {0: 0, 1: 1, 2: 1, 7: 2, 3: 2, 4: 3, 5: 3, 6: 4, 8: 4}
        for l in layers:
            for u in [0, 1, 2, 7, 3, 4, 5, 6, 8]:
                c0 = WIN_UNITS[u]
                cast_unit(l, u, w_in[l][:, c0:c0 + 512], 8, WGRP[u])
            if l == layers[0]:
                bias_setup()
            for mgi in range(2):
                for bi, b in enumerate(MERGE_ORDER):
                    u = 13 + (mgi * 3 + bi) * 2
                    c0 = 4608 + b * 1024 + mgi * 512
                    cast_unit(l, u, w_in[l][:, c0:c0 + 512], 8, 5 + mgi)
                    cast_unit(l, u + 1, w_bo[b][l][:, mgi * 512:(mgi + 1) * 512], 4, 5 + mgi)
            for hf in range(2):
                cast_unit(l, 25 + hf, w_out[l][:, hf * 512:(hf + 1) * 512], 8, 7)
        for l in layers:
            for c in range(4):
                s_ = state["diag_j"] % NRING
                state["diag_j"] += 1
                for k in range(31):
                    ts(DVE, ring[s_][:, k * 128:(k + 1) * 128], ident_sb[:], dwT_sb[:, l, c, k:k + 1], None,
                       ALU.mult, None, [("c", "ident"), ("c", "dwT")], [("ring", s_)])
                dma(SP, wscr[l, 9 + c][:, 0:3968], ring[s_][:, 0:3968], [("ring", s_)], [("wscr", l, 9 + c)],
                    ("diagst", s_))

        tiles = [(s, t) for s in range(NSEQ) for t in range(NT)]
        UORDER = [0, 1, 2, 9, 10, 11, 12, 7, 3, 4, 5, 6, 8] + list(range(13, NU))
        unit_seq = [(l, u) for _ in tiles for l in layers for u in UORDER]

        def emit_loads(upto):
            while state["loaded"] <= min(upto, len(unit_seq) - 1):
                i = state["loaded"]
                l, u = unit_seq[i]
                s_ = i % NRING
                n = unit_cols(u)
                dma(SP, ring[s_][:, 0:n], wscr[l, u][:, 0:n], [("wscr", l, u)], [("ring", s_)], ("ring", s_))
                state["loaded"] += 1

        def use_unit(l, u, extra=0):
            i = state["cursor"]
            assert unit_seq[i] == (l, u), (unit_seq[i], l, u)
            emit_loads(i + NRING - 1 - extra)
            state["cursor"] += 1
            return i % NRING

        def load_x(ti):
            s, t = tiles[ti]
            b = ti % 2
            dma(SP, xbuf[b][:], xT[s][:, t * T:(t + 1) * T].rearrange("(c p) t -> p c t", p=128),
                [], [("x", b, c) for c in range(8)], ("xld", b))

        def prenorm_sq(l, ti, hoisted=False):
            xb_i = ti % 2
            xb = xbuf[xb_i]
            X = lambda c: ("x", xb_i, c)
            sq = [(cbo[:, c, :], ("cbo", c)) for c in range(4)] + [(pmb[:, c, :], ("pmb", c)) for c in range(4)]
            for c in range(8):
                if c % 2 == 0 or not hoisted:
                    act(sq[c][0], xb[:, c, :], AF.Square, [X(c)], [sq[c][1]])
                else:
                    tt(DVE, sq[c][0], xb[:, c, :], xb[:, c, :], ALU.mult, [X(c)], [sq[c][1]])

        def prenorm_fin(l, ti):
            state["prenormed"].add((l, ti))
            xb_i = ti % 2
            xb = xbuf[xb_i]
            X = lambda c: ("x", xb_i, c)
            sq = [(cbo[:, c, :], ("cbo", c)) for c in range(4)] + [(pmb[:, c, :], ("pmb", c)) for c in range(4)]
            bk = bank03()
            for c in range(8):
                mm(ps[bk][:], ones_bf[:], sq[c][0], c == 0, c == 7, [("c", "ones"), sq[c][1]], [("ps", bk)])
            act(rs2[:], ps[bk][:], AF.Ln, [("ps", bk), ("c", "cst")], [("rs2",)], bias=cst[:, 0:1], scale=1.0 / D)
            act(rs2[:], rs2[:], AF.Exp, [("rs2",)], [("rs2",)], scale=-0.5)
            for c in range(8):
                stt(DVE, h[:, c, :], xb[:, c, :], pv[:, l, PV_PRE + c:PV_PRE + c + 1], rs2[:],
                    ALU.mult, ALU.mult, [X(c), ("c", "pv"), ("rs2",)], [("h", c)])

        def prenorm(l, ti, hoisted=False):
            prenorm_sq(l, ti, hoisted)
            prenorm_fin(l, ti)

        def layer(l, ti, last_layer):
            s, t = tiles[ti]
            xb_i = ti % 2
            xb = xbuf[xb_i]
            seg = t % 2
            X = lambda c: ("x", xb_i, c)

            def racc(out, in_, reads, writes, expo=-1.0, scale=None, bias=None):
                act(out, in_, AF.Ln, reads, writes, bias=bias, scale=scale)
                act(out, out, AF.Exp, writes, writes, scale=expo)

            dma(SP, bias_sb[:], bscr[l].rearrange("p (h j) -> p h j", h=8), [("bscr", l)], [("bias",)], "biasld2")

            if (l, ti) not in state["prenormed"]:
                prenorm(l, ti)
            if DEBUG_STOP <= 1:
                if last_layer:
                    dma(SP, yT[s][:, t * T:(t + 1) * T].rearrange("(c p) t -> p c t", p=128), xb[:],
                        [X(c) for c in range(8)], [("y", ti)], ("st", xb_i))
                return

            def win_unit(u, evac):
                s_ = use_unit(l, u)
                for j in range(4):
                    bk = bank()
                    for kc in range(8):
                        mm(ps[bk][:], ring[s_][:, kc * 512 + j * 128:kc * 512 + (j + 1) * 128], h[:, kc, :],
                           kc == 0, kc == 7, [("ring", s_), ("h", kc)], [("ps", bk)])
                    evac(j, bk)

            cp(POOL, u_ext[:, :, 0:30], uh[l][:], [("uh", l)], [("u", c) for c in range(4)])
            cp(POOL, p_ext[:, :, 0:15], ph[l][:], [("ph", l)], [("p", g) for g in range(4)])

            win_unit(0, lambda j, bk: act(pmb[:, j, :], ps[bk][:], AF.Sigmoid, [("ps", bk)], [("pmb", j)]))
            win_unit(1, lambda j, bk: tt(DVE, u_ext[:, j, 30:30 + T], ps[bk][:], pmb[:, j, :], ALU.mult,
                                         [("ps", bk), ("pmb", j)], [("u", j)]))
            win_unit(2, lambda j, bk: act(cg[:, j, :], ps[bk][:], AF.Silu, [("ps", bk)], [("cg", j)]))

            for c in range(4):
                s_ = use_unit(l, 9 + c)
                bk = bank()
                for k in range(31):
                    mm(ps[bk][:], ring[s_][:, k * 128:(k + 1) * 128], u_ext[:, c, k:k + T], k == 0, k == 30,
                       [("ring", s_), ("u", c)], [("ps", bk)])
                dwb = pv[:, l, PV_DWB + c:PV_DWB + c + 1]
                act(big[:, c, :], ps[bk][:], AF.Identity, [("ps", bk), ("c", "pv")], [("big", c)], bias=dwb)
                act(mrg[:, 4 + c, :], ps[bk][:], AF.Square, [("ps", bk), ("c", "pv")], [("mg", 4 + c)], bias=dwb)
                cp(POOL, mrg[:, c, :], big[:, c, :], [("big", c)], [("mg", c)])
            cp(POOL, uh[l][:], u_ext[:, :, T:T + 30], [("u", c) for c in range(4)], [("uh", l)])
            win_unit(7, lambda j, bk: act(p_ext[:, j, 15:15 + T], ps[bk][:], AF.Copy, [("ps", bk)], [("p", j)]))
            for g in range(4):
                w = 2 << g
                cur, oth = pa, pb
                tt(POOL, cur[:, 1:T + 15], p_ext[:, g, 1:T + 15], p_ext[:, g, 0:T + 14], ALU.add,
                   [("p", g)], [("pa",)])
                cur_n, oth_n = "pa", "pb"
                sh = 2
                while sh < w:
                    lo = 2 * sh - 1
                    tt(POOL, oth[:, lo:T + 15], cur[:, lo:T + 15], cur[:, lo - sh:T + 15 - sh], ALU.add,
                       [(cur_n,)], [(oth_n,)])
                    cur, oth = oth, cur
                    cur_n, oth_n = oth_n, cur_n
                    sh *= 2
                stt(DVE, pmb[:, g, :], cur[:, 15:15 + T], 1.0 / w, p_ext[:, g, 15:15 + T], ALU.mult, ALU.subtract,
                    [(cur_n,), ("p", g)], [("pmb", g)])
                if t == 0:
                    tt(POOL, oth[:, 0:w - 1], cur[:, 15:15 + w - 1], invc[:, 0:w - 1], ALU.mult,
                       [(cur_n,), ("c", "invc")], [(oth_n,)])
                    tt(POOL, pmb[:, g, 0:w - 1], oth[:, 0:w - 1], p_ext[:, g, 15:15 + w - 1], ALU.subtract,
                       [(oth_n,), ("p", g), ("pmb", g)], [("pmb", g)])
            cp(POOL, ph[l][:], p_ext[:, :, T:T + 15], [("p", g) for g in range(4)], [("ph", l)])

            if False:
                if last_layer:
                    dma(SP, yT[s][:, t * T:(t + 1) * T].rearrange("(c p) t -> p c t", p=128), xb[:],
                        [X(c) for c in range(8)], [("y", ti)], ("st", xb_i))
                return
            bk1 = bank()
            for c in range(4):
                mm(ps[bk1][:], ones_bf[:], mrg[:, c, :], c == 0, c == 3, [("c", "ones"), ("mg", c)], [("ps", bk1)])
            bk2 = bank()
            for c in range(4):
                mm(ps[bk2][:], ones_bf[:], mrg[:, 4 + c, :], c == 0, c == 3, [("c", "ones"), ("mg", 4 + c)], [("ps", bk2)])

            ts(DVE, big[:, 4, :], ps[bk1][:], 1.0 / 512, None, ALU.mult, None, [("ps", bk1)], [("big", 4)])
            ts(DVE, big[:, 5, :], ps[bk2][:], 1.0 / 512, None, ALU.mult, None, [("ps", bk2)], [("big", 5)])
            if DEBUG_STOP <= 2:
                if last_layer:
                    dma(SP, yT[s][:, t * T:(t + 1) * T].rearrange("(c p) t -> p c t", p=128), xb[:],
                        [X(c) for c in range(8)], [("y", ti)], ("st", xb_i))
                return
            if ti + 1 < len(tiles) and l == layers[0]:
                load_x(ti + 1)
            win_unit(3, lambda j, bk: act(qT[:, j, :], ps[bk][:], AF.Copy, [("ps", bk)], [("q", j, 0), ("q", j, 1)],
                                          scale=0.125))
            win_unit(4, lambda j, bk: act(kT[l][:, j, seg, :], ps[bk][:], AF.Copy, [("ps", bk)], [("k", l, j, seg)]))
            s_ = use_unit(l, 5)
            for tb in range(4):
                bk = bank()
                for kc in range(8):
                    mm(ps[bk][:], h[:, kc, tb * 128:(tb + 1) * 128], ring[s_][:, kc * 512:(kc + 1) * 512],
                       kc == 0, kc == 7, [("ring", s_), ("h", kc)], [("ps", bk)])
                pv4 = ps[bk][:].rearrange("p (i two d) -> p i two d", two=2, d=64)
                act(Vb[l][:, seg, tb, :, 0:64], pv4[:, :, 0, :], AF.Copy, [("ps", bk), ("vones", l)], [("v", l, seg, tb)])
                act(Vb[l][:, seg, tb, :, 128:192], pv4[:, :, 1, :], AF.Copy, [("ps", bk), ("vones", l)], [("v", l, seg, tb)])

            tt(DVE, big[:, 6, :], big[:, 4, :], big[:, 4, :], ALU.mult, [("big", 4)], [("big", 6)])
            stt(DVE, big[:, 5, :], big[:, 5, :], EPS, big[:, 6, :], ALU.add, ALU.subtract, [("big", 5), ("big", 6)], [("big", 5)])
            racc(big[:, 6, :], big[:, 5, :], [("big", 5)], [("big", 6)], expo=-0.5)
            for c in range(4):
                e1 = DVE if c % 2 == 0 else POOL
                tt(e1, big[:, c, :], big[:, c, :], big[:, 4, :], ALU.subtract, [("big", c), ("big", 4)], [("big", c)])
                tt(e1, big[:, c, :], big[:, c, :], big[:, 6, :], ALU.mult, [("big", c), ("big", 6)], [("big", c)])
            for c in range(4):
                e1 = DVE if c % 2 == 0 else POOL
                act(big[:, c, :], big[:, c, :], AF.Silu, [("big", c), ("c", "pv")], [("big", c)],
                    bias=pv[:, l, PV_LNB + c:PV_LNB + c + 1], scale=pv[:, l, PV_LNG + c:PV_LNG + c + 1])
                tt(e1, cbo[:, c, :], big[:, c, :], cg[:, c, :], ALU.mult, [("big", c), ("cg", c)], [("cbo", c)])

            win_unit(6, lambda j, bk: act(ag[:, j, :], ps[bk][:], AF.Silu, [("ps", bk)], [("ag", j)]))
            win_unit(8, lambda j, bk: act(pg[:, j, :], ps[bk][:], AF.Silu, [("ps", bk)], [("pg", j)]))

            def pool_mix():
                for g in range(4):
                    bk = bank()
                    mm(ps[bk][:], pw_bf[:, l, g, :], pmb[:, g, :], True, True, [("c", "pw"), ("pmb", g)], [("ps", bk)])
                    tmp, tmpn = [(rs2[:], ("rs2",)), (sg[0][:], ("sg", 0)), (sg[1][:], ("sg", 1)), (rs[:], ("rs",))][g]
                    ts(DVE, tmp, ps[bk][:], pv[:, l, PV_PB + g:PV_PB + g + 1], pv[:, l, PV_PS + g:PV_PS + g + 1],
                       ALU.add, ALU.mult, [("ps", bk), ("c", "pv")], [tmpn])
                    tt(POOL, pmb[:, g, :], tmp, pg[:, g, :], ALU.mult, [tmpn, ("pg", g)], [("pmb", g)])

            kcs = [kc for kc in range(-8, 8, 2) if (t > 0 or kc >= 0)]
            steps = []
            for i in range(4):
                for kc in kcs:
                    steps.append((2 * i, kc))
                    steps.append((2 * i + 1, kc))
            ns = len(steps)
            first_of = {}
            last_of = {}
            for j, (hd, kc) in enumerate(steps):
                first_of.setdefault(hd, j)
                last_of[hd] = j

            def geom(j):
                hd, kc = steps[j]
                cs, ce = max(0, kc), min(7, kc + 9)
                c0, c1 = cs * 64, (ce + 1) * 64
                j0 = (cs - kc) * 64
                if kc < 0:
                    sgm, koff, blk = 1 - seg, (8 + kc) * 64, (8 + kc) // 2
                else:
                    sgm, koff, blk = seg, kc * 64, kc // 2
                return hd, kc, c0, c1, j0, sgm, koff, blk

            SB6 = [0, 1, 2, 3, 6, 7]

            def emitS(j):
                hd, kc, c0, c1, j0, sgm, koff, blk = geom(j)
                i = hd // 2
                hh = hd % 2
                p0 = hh * 64
                n = c1 - c0
                bk = bank03()
                mm(ps[bk][:, 0:n], kT[l][p0:p0 + 64, i, sgm, koff:koff + 128], qT[p0:p0 + 64, i, c0:c1], True, True,
                   [("k", l, i, sgm), ("q", i, hh)], [("ps", bk)])
                tt(DVE, ps[bk][:, 0:n], ps[bk][:, 0:n], bias_sb[:, hd, j0:j0 + n], ALU.add,
                   [("ps", bk), ("bias",)], [("ps", bk)])
                act(PT[j % NPT][:, 0:n], ps[bk][:, 0:n], AF.Exp, [("ps", bk)], [("PT", j % NPT)])

            def emitPV(j):
                hd, kc, c0, c1, j0, sgm, koff, blk = geom(j)
                i = hd // 2
                hh = hd % 2
                n = c1 - c0
                first = first_of[hd] == j
                last = last_of[hd] == j
                ob = 4 + (i % 2) * 2 + hh
                rd = [("v", l, sgm, blk), ("PT", j % NPT)]
                pt = PT[j % NPT][:, 0:n]
                vl = Vb[l][:, sgm, blk, i, 0:128] if hh == 0 else Vb[l][:, sgm, blk, i, 64:192]
                mm(ps[ob][:, c0:c1], vl, pt, first, last, rd, [("ps", ob)], skip_group_check=True)
                if last:
                    pending.append((j + 1, lambda: normalize(hd, i, hh, ob)))

            def normalize(hd, i, hh, ob):
                o_lo, o_hi = (0, 64) if hh == 0 else (64, 128)
                d_lo, d_hi = (64, 128) if hh == 0 else (0, 64)
                cp(DVE, big[o_lo:o_hi, i, :], ps[ob][o_lo:o_hi, :], [("ps", ob)], [("big", i)])
                cp(DVE, big[o_lo:o_hi, 4 + i, :], ps[ob][d_lo:d_hi, :], [("ps", ob)], [("big", 4 + i)])

            def finish_attention():
                for i in range(4):
                    dn = big[:, 4 + i, :]
                    rD = [("big", 4 + i)]
                    act(dn, dn, AF.Ln, rD, rD)
                    act(dn, dn, AF.Exp, rD, rD, scale=-1.0)
                    tt(DVE, big[:, i, :], big[:, i, :], dn, ALU.mult, [("big", i)] + rD, [("big", i)])
                    tt(POOL, qT[:, i, :], big[:, i, :], ag[:, i, :], ALU.mult,
                       [("big", i), ("ag", i)], [("q", i, 0), ("q", i, 1)])

            pending = []
            pool_mix()
            for j0_ in range(0, min(4, ns)):
                emitS(j0_)
            for j in range(0, ns, 2):
                if j + 4 < ns:
                    emitS(j + 4)
                    emitS(j + 5)
                emitPV(j)
                emitPV(j + 1)
                for due, fn in [p for p in pending if p[0] <= j]:
                    fn()
                pending[:] = [p for p in pending if p[0] > j]
            for due, fn in pending:
                fn()
            finish_attention()

            if DEBUG_STOP <= 4:
                if last_layer:
                    dma(SP, yT[s][:, t * T:(t + 1) * T].rearrange("(c p) t -> p c t", p=128), xb[:],
                        [X(c) for c in range(8)], [("y", ti)], ("st", xb_i))
                return
            br = {0: (cbo, lambda kc: [("cbo", kc)]),
                  1: (qT, lambda kc: [("q", kc, 0), ("q", kc, 1)]),
                  2: (pmb, lambda kc: [("pmb", kc)])}
            for mgi in range(2):
                for bi, b in enumerate(MERGE_ORDER):
                    u = 13 + (mgi * 3 + bi) * 2
                    sG = use_unit(l, u)
                    sB = use_unit(l, u + 1, extra=1)
                    bb, bregs = br[b]
                    hoist = last_layer and ti + 1 < len(tiles) and mgi == 1 and bi == 2
                    if hoist:
                        prenorm_sq(layers[0], ti + 1, hoisted=True)
                    for mi in range(4):
                        m = mgi * 4 + mi
                        bkG = bank()
                        for kc in range(8):
                            mm(ps[bkG][:], ring[sG][:, kc * 512 + mi * 128:kc * 512 + (mi + 1) * 128], h[:, kc, :],
                               kc == 0, kc == 7, [("ring", sG), ("h", kc)], [("ps", bkG)])
                        bkY = bank()
                        for kc in range(4):
                            mm(ps[bkY][:], ring[sB][:, kc * 512 + mi * 128:kc * 512 + (mi + 1) * 128], bb[:, kc, :],
                               kc == 0, kc == 3, [("ring", sB)] + bregs(kc), [("ps", bkY)])
                        si = state["sg"] % 2
                        state["sg"] += 1
                        act(sg[si][:], ps[bkG][:], AF.Sigmoid, [("ps", bkG)], [("sg", si)])
                        if bi == 0:
                            tt(DVE, big[:, mi, :], ps[bkY][:], sg[si][:], ALU.mult, [("ps", bkY), ("sg", si)],
                               [("big", mi)])
                        else:
                            tt(DVE, sg[si][:], ps[bkY][:], sg[si][:], ALU.mult, [("ps", bkY), ("sg", si)], [("sg", si)])
                            if bi == 1:
                                tt(POOL, big[:, mi, :], big[:, mi, :], sg[si][:], ALU.add, [("big", mi), ("sg", si)],
                                   [("big", mi)])
                            else:
                                tt(POOL, mrg[:, m, :], big[:, mi, :], sg[si][:], ALU.add, [("big", mi), ("sg", si)],
                                   [("mg", m)])
                    if hoist:
                        prenorm_fin(layers[0], ti + 1)

            if DEBUG_STOP <= 5:
                if last_layer:
                    dma(SP, yT[s][:, t * T:(t + 1) * T].rearrange("(c p) t -> p c t", p=128), xb[:],
                        [X(c) for c in range(8)], [("y", ti)], ("st", xb_i))
                return
            sq2 = [(cg[:, c, :], ("cg", c)) for c in range(4)] + [(ag[:, c, :], ("ag", c)) for c in range(4)]
            for hf in range(2):
                s_ = use_unit(l, 25 + hf)
                for mi in range(4):
                    m = hf * 4 + mi
                    bk = bank()
                    for kc in range(8):
                        mm(ps[bk][:], ring[s_][:, kc * 512 + mi * 128:kc * 512 + (mi + 1) * 128], mrg[:, kc, :],
                           kc == 0, kc == 7, [("ring", s_), ("mg", kc)], [("ps", bk)])
                    act(sq2[m][0], ps[bk][:], AF.Square, [("ps", bk)], [sq2[m][1]])
                    ts(DVE, big[:, m, :], ps[bk][:], pv[:, l, PV_POST + m:PV_POST + m + 1], None, ALU.mult, None,
                       [("ps", bk), ("c", "pv")], [("big", m)])
            if DEBUG_STOP <= 6:
                if last_layer:
                    dma(SP, yT[s][:, t * T:(t + 1) * T].rearrange("(c p) t -> p c t", p=128), xb[:],
                        [X(c) for c in range(8)], [("y", ti)], ("st", xb_i))
                return
            bk = bank()
            for c in range(8):
                mm(ps[bk][:], ones_bf[:], sq2[c][0], c == 0, c == 7, [("c", "ones"), sq2[c][1]], [("ps", bk)])
            act(rs[:], ps[bk][:], AF.Ln, [("ps", bk), ("c", "cst")], [("rs",)], bias=cst[:, 0:1], scale=1.0 / D)
            act(rs[:], rs[:], AF.Exp, [("rs",)], [("rs",)], scale=-0.5)
            for c in range(8):
                tt(DVE, big[:, c, :], big[:, c, :], rs[:], ALU.mult, [("big", c), ("rs",)], [("big", c)])
                tt(DVE, xb[:, c, :], xb[:, c, :], big[:, c, :], ALU.add, [X(c), ("big", c)], [X(c)])
            if last_layer:
                dma(SP, yT[s][:, t * T:(t + 1) * T].rearrange("(c p) t -> p c t", p=128), xb[:],
                    [X(c) for c in range(8)], [("y", ti)], ("st", xb_i))

        load_x(0)
        for ti, (s, t) in enumerate(tiles):
            if t == 0:
                for l in layers:
                    mset(POOL, uh[l][:], 0.0, [("uh", l)])
                    mset(POOL, ph[l][:], 0.0, [("ph", l)])
            for li, l in enumerate(layers):
                layer(l, ti, li == len(layers) - 1)
        fw = [("st", 0)] + ([("st", 1)] if len(tiles) > 1 else [])
        S.emit(nc, final_wait_streams=fw)
    return nc, S


def prep_shared(inp):
    f = lambda k: np.ascontiguousarray(np.asarray(inp[k], dtype=np.float32))

    def cols(v, n):
        return np.ascontiguousarray(v.reshape(L, n, 128).transpose(2, 0, 1))

    pvec = np.concatenate([
        cols(f("pre_norm_g"), 8), cols(f("post_norm_g"), 8), cols(f("conv_dw_b"), 4), cols(f("conv_ln_g"), 4),
        cols(f("conv_ln_b"), 4), cols(f("pool_b").reshape(L, 512), 4), cols(f("pool_scale"), 4)], axis=2)
    dw = f("conv_dw")
    dwT = np.ascontiguousarray(dw.reshape(L, 31, 4, 128).transpose(3, 0, 2, 1))
    rb = f("rel_bias")
    p = np.arange(128)[:, None]
    j = np.arange(640)[None, :]
    idx = np.clip(j - p, -256, 256) + 256
    biasT = np.ascontiguousarray(rb[:, :, idx].transpose(2, 0, 1, 3))
    ident = np.eye(128, dtype=np.float32)
    invcnt = np.ascontiguousarray(np.broadcast_to((1.0 / np.arange(1, 17, dtype=np.float32))[None, :], (128, 16)))
    return {
        "w_in": f("w_in"), "w_co": f("w_conv_out"), "w_ao": f("w_attn_out"), "w_po": f("w_pool_out"),
        "w_out": f("w_out"), "pool_w": f("pool_w"), "pvec": np.ascontiguousarray(pvec), "dwT": dwT,
        "biasT": biasT, "ident": ident, "invcnt": invcnt,
    }


_CACHE = {}


def run_model(inp, NSEQ, NT, layers=(0, 1)):
    x = np.asarray(inp["x"], dtype=np.float32)
    B, SL, _ = x.shape
    assert B == N_CORES * NSEQ and SL == NT * T
    key = (NSEQ, NT, tuple(layers))
    if key not in _CACHE:
        _CACHE[key] = build_program(NSEQ, NT, layers)[0]
    nc = _CACHE[key]
    shared = prep_shared(inp)
    in_maps = []
    for c in range(N_CORES):
        m = dict(shared)
        m["xT"] = np.ascontiguousarray(x[c * NSEQ:(c + 1) * NSEQ].transpose(0, 2, 1))
        in_maps.append(m)
    res = run_bass_kernel_spmd(nc, in_maps, core_ids=list(range(N_CORES)))
    out = np.empty((B, SL, D), dtype=np.float32)
    for c in range(N_CORES):
        out[c * NSEQ:(c + 1) * NSEQ] = res.results[c]["yT"].transpose(0, 2, 1)
    return out


def kernel(**inputs):
    return run_model(inputs, 4, 4, (0, 1))
```
